# Optimizing a Trainium2 kernel written in Bass

```python
import jax, jax.numpy as jnp
from jax import lax
import numpy as np

D_MODEL = 1024
BATCH = 16
SEQ = 2048
DEPTH = 2

N_A_LAYERS = DEPTH // 2
N_B_LAYERS = DEPTH - N_A_LAYERS
LRU_WIDTH = D_MODEL
LRU_HEADS = 8
LRU_BLOCK = LRU_WIDTH // LRU_HEADS
CONV_WIDTH = 4
LRU_C = 8.0
ATTN_HEADS = 16
ATTN_HEAD_DIM = 64
ATTN_WIDTH = ATTN_HEADS * ATTN_HEAD_DIM
Q_BLOCK = 128
PEER_HEADS = 8
PEER_NKEYS = 128
PEER_EXPERTS = PEER_NKEYS * PEER_NKEYS
PEER_TOPK = 16
PEER_QDIM = 256
PEER_HALF = PEER_QDIM // 2
PEER_CHUNK = 128
PLE_DIM = 256
EPS = 1e-6

kernel_name = 'yoco_rglru_stickbreak_peer'


def rms_norm(x, g):
    xf = x.astype(jnp.float32)
    y = xf * lax.rsqrt(jnp.mean(xf * xf, axis=-1, keepdims=True) + EPS)
    return (y * g.astype(jnp.float32)).astype(x.dtype)


def rglru_block(h, w_in, conv_w, conv_b, w_r, w_i, b_r, b_i, lam, w_out):
    B, S, _ = h.shape
    proj = h @ w_in
    y_branch, x_branch = jnp.split(proj, 2, axis=-1)
    xc = lax.conv_general_dilated(x_branch, conv_w[:, None, :], window_strides=(1,),
                                  padding=((CONV_WIDTH - 1, 0),),
                                  dimension_numbers=('NWC', 'WIO', 'NWC'),
                                  feature_group_count=LRU_WIDTH) + conv_b
    xb = xc.reshape(B, S, LRU_HEADS, LRU_BLOCK)
    r = jax.nn.sigmoid(jnp.einsum('bshi,hij->bshj', xb, w_r).reshape(B, S, LRU_WIDTH).astype(jnp.float32)
                       + b_r.astype(jnp.float32))
    ig = jax.nn.sigmoid(jnp.einsum('bshi,hij->bshj', xb, w_i).reshape(B, S, LRU_WIDTH).astype(jnp.float32)
                        + b_i.astype(jnp.float32))
    log_a = -LRU_C * r * jax.nn.softplus(-lam.astype(jnp.float32))
    a = jnp.exp(log_a)
    u = jnp.sqrt(-jnp.expm1(2.0 * log_a)) * ig * xc.astype(jnp.float32)

    def combine(left, right):
        a1, b1 = left
        a2, b2 = right
        return a1 * a2, a2 * b1 + b2

    _, hseq = lax.associative_scan(combine, (a, u), axis=1)
    gated = (jax.nn.gelu(y_branch.astype(jnp.float32)) * hseq).astype(h.dtype)
    return gated @ w_out


def stick_breaking_attention(q, k, v):
    S = q.shape[1]
    scale = ATTN_HEAD_DIM ** -0.5
    outs = []
    for blk in range(S // Q_BLOCK):
        q0 = blk * Q_BLOCK
        end = q0 + Q_BLOCK
        qb = q[:, q0:end].astype(jnp.float32)
        kb = k[:, :end].astype(jnp.float32)
        z = jnp.einsum('bqhd,bkhd->bhqk', qb, kb) * scale
        t_pos = q0 + jnp.arange(Q_BLOCK)[:, None]
        s_pos = jnp.arange(end)[None, :]
        causal = s_pos < t_pos
        lbeta = jax.nn.log_sigmoid(z)
        l1m = jnp.where(causal, lbeta - z, 0.0)
        rc = lax.cumsum(l1m, axis=3, reverse=True)
        A = jnp.where(causal, jnp.exp(lbeta + rc - l1m), 0.0)
        outs.append(jnp.einsum('bhqk,bkhd->bqhd', A, v[:, :end].astype(jnp.float32)))
    return jnp.concatenate(outs, axis=1).astype(v.dtype)


def peer_ffn(h, w_q, sub_keys, u_tab, v_tab):
    B, S, D = h.shape
    T = B * S
    xt = h.reshape(T, D)
    q = (xt @ w_q).reshape(T, PEER_HEADS, 2, PEER_HALF).astype(jnp.float32)
    s = jnp.einsum('thpd,hpnd->thpn', q, sub_keys.astype(jnp.float32))
    sv, si = lax.top_k(s, PEER_TOPK)
    cand = sv[:, :, 0, :, None] + sv[:, :, 1, None, :]
    cv, ci = lax.top_k(cand.reshape(T, PEER_HEADS, PEER_TOPK * PEER_TOPK), PEER_TOPK)
    i1 = jnp.take_along_axis(si[:, :, 0], ci // PEER_TOPK, axis=-1)
    i2 = jnp.take_along_axis(si[:, :, 1], ci % PEER_TOPK, axis=-1)
    eidx = (i1 * PEER_NKEYS + i2).reshape(T, PEER_HEADS * PEER_TOPK)
    gates = jax.nn.softmax(cv, axis=-1).reshape(T, PEER_HEADS * PEER_TOPK)
    n_chunks = T // PEER_CHUNK

    def chunk_fn(args):
        xc, ic, gc = args
        uc = jnp.take(u_tab, ic, axis=0)
        vc = jnp.take(v_tab, ic, axis=0)
        act = jax.nn.gelu(jnp.einsum('cd,ckd->ck', xc, uc).astype(jnp.float32))
        return jnp.einsum('ck,ckd->cd', (gc * act).astype(vc.dtype), vc)

    out = lax.map(chunk_fn, (xt.reshape(n_chunks, PEER_CHUNK, D),
                             eidx.reshape(n_chunks, PEER_CHUNK, PEER_HEADS * PEER_TOPK),
                             gates.reshape(n_chunks, PEER_CHUNK, PEER_HEADS * PEER_TOPK)))
    return out.reshape(B, S, D).astype(h.dtype)


def per_layer_embed(h, p_i, g, w_gate, w_proj):
    gate = jax.nn.sigmoid((rms_norm(h, g) @ w_gate).astype(jnp.float32))
    return h + (gate * (p_i @ w_proj).astype(jnp.float32)).astype(h.dtype)


def setup_inputs(seed: int = 0) -> dict:
    key = jax.random.key(seed)
    ks = jax.random.split(key, 28)
    nrm = jax.random.normal
    f32 = jnp.float32
    a8 = jax.random.uniform(ks[9], (N_A_LAYERS, LRU_WIDTH), f32, 0.9, 0.999)
    a0 = a8 ** (1.0 / LRU_C)
    return {
        'x': nrm(ks[0], (BATCH, SEQ, D_MODEL), f32),
        'p': nrm(ks[1], (DEPTH, BATCH, SEQ, PLE_DIM), f32),
        'norm_mix': 1.0 + 0.02 * nrm(ks[2], (DEPTH, D_MODEL), f32),
        'a_w_in': nrm(ks[3], (N_A_LAYERS, D_MODEL, 2 * LRU_WIDTH), f32) * D_MODEL ** -0.5,
        'a_conv_w': nrm(ks[4], (N_A_LAYERS, CONV_WIDTH, LRU_WIDTH), f32) * CONV_WIDTH ** -0.5,
        'a_conv_b': 0.02 * nrm(ks[5], (N_A_LAYERS, LRU_WIDTH), f32),
        'a_w_r': nrm(ks[6], (N_A_LAYERS, LRU_HEADS, LRU_BLOCK, LRU_BLOCK), f32) * LRU_BLOCK ** -0.5,
        'a_w_i': nrm(ks[7], (N_A_LAYERS, LRU_HEADS, LRU_BLOCK, LRU_BLOCK), f32) * LRU_BLOCK ** -0.5,
        'a_b_r': 0.02 * nrm(ks[8], (N_A_LAYERS, LRU_WIDTH), f32),
        'a_b_i': 0.02 * nrm(ks[10], (N_A_LAYERS, LRU_WIDTH), f32),
        'a_lambda': jnp.log(a0) - jnp.log1p(-a0),
        'a_w_out': nrm(ks[11], (N_A_LAYERS, LRU_WIDTH, D_MODEL), f32) * LRU_WIDTH ** -0.5,
        'kv_norm': 1.0 + 0.02 * nrm(ks[12], (D_MODEL,), f32),
        'w_kv': nrm(ks[13], (D_MODEL, 2 * ATTN_WIDTH), f32) * D_MODEL ** -0.5,
        'b_w_q': nrm(ks[14], (N_B_LAYERS, D_MODEL, ATTN_WIDTH), f32) * D_MODEL ** -0.5,
        'b_w_o': nrm(ks[15], (N_B_LAYERS, ATTN_WIDTH, D_MODEL), f32) * ATTN_WIDTH ** -0.5,
        'norm_ffn': 1.0 + 0.02 * nrm(ks[16], (DEPTH, D_MODEL), f32),
        'peer_w_q': nrm(ks[17], (DEPTH, D_MODEL, PEER_HEADS * PEER_QDIM), f32) * D_MODEL ** -0.5,
        'peer_sub_keys': nrm(ks[18], (DEPTH, PEER_HEADS, 2, PEER_NKEYS, PEER_HALF), f32) * PEER_HALF ** -0.5,
        'peer_u': nrm(ks[19], (DEPTH, PEER_EXPERTS, D_MODEL), f32) * D_MODEL ** -0.5,
        'peer_v': nrm(ks[20], (DEPTH, PEER_EXPERTS, D_MODEL), f32) * (PEER_HEADS * PEER_TOPK) ** -0.5,
        'norm_ple': 1.0 + 0.02 * nrm(ks[21], (DEPTH, D_MODEL), f32),
        'ple_w_gate': nrm(ks[22], (DEPTH, D_MODEL, D_MODEL), f32) * D_MODEL ** -0.5,
        'ple_w_proj': nrm(ks[23], (DEPTH, PLE_DIM, D_MODEL), f32) * PLE_DIM ** -0.5,
        'final_norm': 1.0 + 0.02 * nrm(ks[24], (D_MODEL,), f32),
    }


def reference(x, p, norm_mix, a_w_in, a_conv_w, a_conv_b, a_w_r, a_w_i, a_b_r, a_b_i, a_lambda,
              a_w_out, kv_norm, w_kv, b_w_q, b_w_o, norm_ffn, peer_w_q, peer_sub_keys, peer_u,
              peer_v, norm_ple, ple_w_gate, ple_w_proj, final_norm):
    B, S, _ = x.shape
    h = x
    k_shared = None
    v_shared = None
    for i in range(DEPTH):
        hn = rms_norm(h, norm_mix[i])
        if i < N_A_LAYERS:
            j = i
            h = h + rglru_block(hn, a_w_in[j], a_conv_w[j], a_conv_b[j], a_w_r[j], a_w_i[j],
                                a_b_r[j], a_b_i[j], a_lambda[j], a_w_out[j])
        else:
            j = i - N_A_LAYERS
            if i == N_A_LAYERS:
                kvp = rms_norm(h, kv_norm) @ w_kv
                k_flat, v_flat = jnp.split(kvp, 2, axis=-1)
                k_shared = k_flat.reshape(B, S, ATTN_HEADS, ATTN_HEAD_DIM)
                v_shared = v_flat.reshape(B, S, ATTN_HEADS, ATTN_HEAD_DIM)
            q = (hn @ b_w_q[j]).reshape(B, S, ATTN_HEADS, ATTN_HEAD_DIM)
            o = stick_breaking_attention(q, k_shared, v_shared).reshape(B, S, ATTN_WIDTH)
            h = h + o @ b_w_o[j]
        h = h + peer_ffn(rms_norm(h, norm_ffn[i]), peer_w_q[i], peer_sub_keys[i], peer_u[i], peer_v[i])
        h = per_layer_embed(h, p[i], norm_ple[i], ple_w_gate[i], ple_w_proj[i])
    return rms_norm(h, final_norm)
```

```python
import numpy as np
import concourse.bass as bass
import concourse.mybir as mybir
from concourse.bass import AP
from concourse.bass_utils import run_bass_kernel_spmd

F32 = mybir.dt.float32
BF16 = mybir.dt.bfloat16
U32 = mybir.dt.uint32
AF = mybir.ActivationFunctionType
ALU = mybir.AluOpType
AX = mybir.AxisListType

NCORES = 8
BPC = 2
S = 2048
D = 1024
T = BPC * S
NT = T // 128
PLE = 256
NKEY = 128
ARENA_WORDS = 52000
KDMA = 12
EPS = 1e-6
DSIZE = {F32: 4, BF16: 2, U32: 4}


class Buf:
    __slots__ = ("w", "r")

    def __init__(self):
        self.w = {}
        self.r = {}


class Tl:
    def __init__(self, ap, b=None):
        self.ap = ap
        self.b = b if b is not None else Buf()


class Dram:
    def __init__(self, ap, ntiles):
        self.ap = ap
        self.bufs = [Buf() for _ in range(ntiles)]


class Prog:
    CE = ("pe", "act", "dve", "pool")
    DQ = ("sp", "pool", "act")

    def __init__(self, nc):
        self.nc = nc
        self.eng = {"pe": nc.tensor, "act": nc.scalar, "dve": nc.vector, "pool": nc.gpsimd, "sp": nc.sync}
        self.semh = {}
        for e in self.CE:
            self.semh[e] = nc.alloc_semaphore("c_" + e)
        for q in self.DQ:
            for k in range(KDMA):
                self.semh["%s_d%d" % (q, k)] = nc.alloc_semaphore("d_%s_%d" % (q, k))
        self.cnt = {e: 0 for e in self.CE}
        self.dcnt = {q: 0 for q in self.DQ}
        self.seen = {e: {} for e in self.eng}
        self.arena = nc.alloc_sbuf_tensor("arena", [128, ARENA_WORDS], F32).ap()
        self.aoff = 0
        self.psb = [Tl(nc.alloc_psum_tensor("psb%d" % i, [128, 512], F32).ap()) for i in range(8)]
        self.ninst = 0

    def tile(self, shape, dtype=F32):
        n = 1
        for s_ in shape[1:]:
            n *= s_
        nbytes = (n * DSIZE[dtype] + 31) // 32 * 32
        words = nbytes // 4
        off = self.aoff
        self.aoff += words
        assert self.aoff <= ARENA_WORDS, ("arena overflow", self.aoff)
        v = self.arena[0:shape[0], off:off + n * DSIZE[dtype] // 4]
        if dtype != F32:
            v = v.bitcast(dtype)
        if len(shape) == 3:
            v = v.rearrange("p (a b) -> p a b", a=shape[1], b=shape[2])
        elif len(shape) == 4:
            v = v.rearrange("p (a b c) -> p a b c", a=shape[1], b=shape[2], c=shape[3])
        return Tl(v)

    def tiles(self, n, shape, dtype=F32):
        return [self.tile(shape, dtype) for _ in range(n)]

    def _wait(self, eng, sem, val):
        if val <= 0 or self.seen[eng].get(sem, 0) >= val:
            return
        self.seen[eng][sem] = val
        self.eng[eng].wait_ge(self.semh[sem], val)

    def _deps(self, eng, R, W):
        need = {}
        for b in R:
            for s_, v in b.w.items():
                if need.get(s_, 0) < v:
                    need[s_] = v
        for b in W:
            for s_, v in b.w.items():
                if need.get(s_, 0) < v:
                    need[s_] = v
            for s_, v in b.r.items():
                if need.get(s_, 0) < v:
                    need[s_] = v
        for s_, v in need.items():
            if eng == "pe" and s_ == "pe":
                continue
            self._wait(eng, s_, v)

    def _mark(self, sem, val, R, W):
        for b in W:
            b.w = {sem: val}
            b.r = {}
        for b in R:
            if b in W:
                continue
            if b.r.get(sem, 0) < val:
                b.r[sem] = val

    def op(self, eng, fn, R=(), W=()):
        self._deps(eng, R, W)
        self.cnt[eng] += 1
        fn(self.eng[eng]).then_inc(self.semh[eng], 1)
        self._mark(eng, self.cnt[eng], R, W)
        self.ninst += 1

    def dma(self, q, out, in_, R=(), W=(), **kw):
        self._deps(q, R, W)
        i = self.dcnt[q]
        self.dcnt[q] += 1
        k = i % KDMA
        val = 16 * (i // KDMA + 1)
        sem = "%s_d%d" % (q, k)
        self._wait(q, sem, val - 16)
        self.eng[q].dma_start(out=out, in_=in_, **kw).then_inc(self.semh[sem], 16)
        self._mark(sem, val, R, W)
        self.ninst += 1

    def barrier(self):
        for e in self.eng:
            for s_ in self.CE:
                self._wait(e, s_, self.cnt[s_])
            for q in self.DQ:
                for k in range(KDMA):
                    i = self.dcnt[q]
                    n = i // KDMA + (1 if k < i % KDMA else 0)
                    self._wait(e, "%s_d%d" % (q, k), 16 * n)

    def stage_begin(self):
        self.barrier()
        self.aoff = 0

    def mm(self, out, lhsT, rhs, start, stop, R, W):
        self.op("pe", lambda e: e.matmul(out, lhsT=lhsT, rhs=rhs, start=start, stop=stop), R, W)

    def tr(self, out, in_, ident, R, W):
        self.op("pe", lambda e: e.transpose(out, in_, ident), R, W)

    def act(self, out, in_, func, R, W, **kw):
        self.op("act", lambda e: e.activation(out=out, in_=in_, func=func, **kw), R, W)

    def tt(self, eng, out, in0, in1, op, R, W):
        self.op(eng, lambda e: e.tensor_tensor(out=out, in0=in0, in1=in1, op=op), R, W)

    def ts(self, eng, out, in0, s1, s2, op0, op1, R, W):
        if op1 is None:
            self.op(eng, lambda e: e.tensor_scalar(out=out, in0=in0, scalar1=s1, scalar2=None, op0=op0), R, W)
        else:
            self.op(eng, lambda e: e.tensor_scalar(out=out, in0=in0, scalar1=s1, scalar2=s2, op0=op0, op1=op1), R, W)

    def stt(self, eng, out, in0, scalar, in1, op0, op1, R, W):
        self.op(eng, lambda e: e.scalar_tensor_tensor(out=out, in0=in0, scalar=scalar, in1=in1, op0=op0, op1=op1), R, W)

    def cp(self, eng, out, in_, R, W):
        if eng == "act":
            self.op("act", lambda e: e.activation(out=out, in_=in_, func=AF.Copy), R, W)
        else:
            self.op(eng, lambda e: e.tensor_copy(out=out, in_=in_), R, W)

    def memset(self, eng, ap, val, W):
        self.op(eng, lambda e: e.memset(ap, val), (), W)


def bcast_rows(dram_ap_1d, n):
    return AP(dram_ap_1d.tensor, dram_ap_1d.offset, [[0, 128], [1, n]])


class Common:
    pass


def load_consts(P, consts):
    c = {}
    ident = P.tile([128, 128], BF16)
    P.dma("pool", ident.ap, consts[:, 0:128], W=[ident.b])
    c["ident"] = ident
    one = P.tile([128, 1], F32)
    P.memset("dve", one.ap, 1.0, [one.b])
    c["one"] = one
    eps = P.tile([128, 1], F32)
    P.memset("dve", eps.ap, EPS, [eps.b])
    c["eps"] = eps
    return c


def rms_rstd(P, c, x_ap, xb, junk, ss, rstd, col):
    P.act(junk.ap, x_ap, AF.Square, R=[xb], W=[junk.b, ss.b], accum_out=ss.ap[:, col:col + 1])
    P.act(rstd.ap[:, col:col + 1], ss.ap[:, col:col + 1], AF.Sqrt, R=[ss.b, c["eps"].b], W=[rstd.b],
          scale=1.0 / D, bias=c["eps"].ap)
    P.op("dve", lambda e: e.reciprocal(out=rstd.ap[:, col:col + 1], in_=rstd.ap[:, col:col + 1]), [rstd.b], [rstd.b])


def transpose_tile(P, c, src, dst_ap, dstb, nk, psT, evac_eng):
    pv = psT.ap.bitcast(BF16)
    for kc in range(nk):
        P.tr(pv[:, kc * 128:(kc + 1) * 128], src.ap[:, kc * 128:(kc + 1) * 128], c["ident"].ap,
             R=[src.b, c["ident"].b], W=[psT.b])
    P.cp(evac_eng, dst_ap, pv[:, 0:nk * 128].rearrange("p (k t) -> p k t", k=nk), [psT.b], [dstb])


def stage_rglru(P, I, hin, hout):
    nc = P.nc
    P.stage_begin()
    c = load_consts(P, I["consts"])
    w_in = P.tile([128, 8, 2048], BF16)
    P.dma("pool", w_in.ap, I["a_w_in"][0].rearrange("(kc p) n -> p kc n", p=128), W=[w_in.b])
    w_out = P.tile([128, 8, 1024], BF16)
    P.dma("pool", w_out.ap, I["a_w_out"][0].rearrange("(kc p) n -> p kc n", p=128), W=[w_out.b])
    w_r = P.tile([128, 8, 128], BF16)
    P.dma("pool", w_r.ap, I["a_w_r"][0].rearrange("h i j -> i h j"), W=[w_r.b])
    w_i = P.tile([128, 8, 128], BF16)
    P.dma("pool", w_i.ap, I["a_w_i"][0].rearrange("h i j -> i h j"), W=[w_i.b])
    g_bc = P.tile([128, 1024], F32)
    P.dma("sp", g_bc.ap, bcast_rows(I["norm_mix"][0], D), W=[g_bc.b])
    cw = P.tile([128, 4, 8], F32)
    P.dma("sp", cw.ap, I["a_conv_w"][0].rearrange("k (c p) -> p k c", p=128), W=[cw.b], allow_slow_non_contiguous=True)
    pv = P.tile([128, 4, 8], F32)
    for i, nm in enumerate(["a_conv_b", "a_b_r", "a_b_i", "a_lambda"]):
        P.dma("sp", pv.ap[:, i, :], I[nm][0].rearrange("(c p) -> p c", p=128), W=[pv.b], allow_slow_non_contiguous=True)
    clam = P.tile([128, 8], F32)
    tmp8 = P.tile([128, 8], F32)
    P.act(tmp8.ap, pv.ap[:, 3, :], AF.Exp, R=[pv.b], W=[tmp8.b], scale=-1.0)
    P.act(tmp8.ap, tmp8.ap, AF.Ln, R=[tmp8.b, c["one"].b], W=[tmp8.b], bias=c["one"].ap)
    P.ts("dve", clam.ap, tmp8.ap, -8.0, None, ALU.mult, None, [tmp8.b], [clam.b])

    hts = P.tiles(2, [128, 4, 1024], F32)
    hns = P.tiles(2, [128, 1024], BF16)
    hnTs = P.tiles(2, [128, 8, 512], BF16)
    junk = P.tile([128, 1024], BF16)
    ss = P.tile([128, 4], F32)
    rstd = P.tile([128, 4], F32)
    gys = P.tiles(2, [128, 512], F32)
    xpads = P.tiles(2, [128, 515], F32)
    xcarry = P.tile([128, 8, 3], F32)
    hprev = P.tile([128, 8], F32)
    xcs = P.tiles(2, [128, 512], F32)
    xcbs = P.tiles(2, [128, 512], BF16)
    rs = P.tiles(2, [128, 512], F32)
    igs = P.tiles(2, [128, 512], F32)
    as_ = P.tiles(2, [128, 512], F32)
    ms = P.tiles(2, [128, 512], F32)
    us = P.tiles(2, [128, 512], F32)
    hss = P.tiles(2, [128, 512], F32)
    gTs = P.tiles(2, [128, 8, 512], BF16)
    psT = [P.psb[0], P.psb[1]]
    psP = [P.psb[2], P.psb[3]]
    psG = [P.psb[4], P.psb[5]]
    psO = [P.psb[6], P.psb[7]]
    it = 0
    pj = 0
    for s_ in range(BPC):
        P.memset("dve", xcarry.ap, 0.0, [xcarry.b])
        P.memset("dve", hprev.ap, 0.0, [hprev.b])
        for ch in range(4):
            r0 = s_ * S + ch * 512
            ti0 = r0 // 128
            ht = hts[it % 2]
            hnT = hnTs[it % 2]
            gT = gTs[it % 2]
            P.dma("sp", ht.ap, hin.ap[r0:r0 + 512, :].rearrange("(tt p) d -> p tt d", p=128),
                  R=hin.bufs[ti0:ti0 + 4], W=[ht.b])
            for tt in range(4):
                hn = hns[tt % 2]
                rms_rstd(P, c, ht.ap[:, tt, :], ht.b, junk, ss, rstd, tt)
                P.stt("dve", hn.ap, ht.ap[:, tt, :], rstd.ap[:, tt:tt + 1], g_bc.ap, ALU.mult, ALU.mult,
                      [ht.b, rstd.b, g_bc.b], [hn.b])
                transpose_tile(P, c, hn, hnT.ap[:, :, tt * 128:(tt + 1) * 128], hnT.b, 8, psT[tt % 2], "act")
            for j in range(8):
                gy = gys[pj % 2]
                xpad = xpads[pj % 2]
                xc = xcs[pj % 2]
                xcb = xcbs[pj % 2]
                r_ = rs[pj % 2]
                ig = igs[pj % 2]
                a_ = as_[pj % 2]
                m_ = ms[pj % 2]
                u_ = us[pj % 2]
                hs = hss[pj % 2]
                pg0 = psG[0]
                pg1 = psG[1]
                py = psP[0]
                for kc in range(8):
                    P.mm(py.ap, w_in.ap[:, kc, j * 128:(j + 1) * 128], hnT.ap[:, kc, :], kc == 0, kc == 7,
                         [w_in.b, hnT.b], [py.b])
                P.act(gy.ap, py.ap, AF.Gelu_apprx_tanh, R=[py.b], W=[gy.b])
                px = psP[1]
                for kc in range(8):
                    P.mm(px.ap, w_in.ap[:, kc, 1024 + j * 128:1024 + (j + 1) * 128], hnT.ap[:, kc, :], kc == 0, kc == 7,
                         [w_in.b, hnT.b], [px.b])
                P.cp("dve", xpad.ap[:, 0:3], xcarry.ap[:, j, :], [xcarry.b], [xpad.b])
                P.cp("act", xpad.ap[:, 3:515], px.ap, [px.b], [xpad.b])
                P.cp("dve", xcarry.ap[:, j, :], xpad.ap[:, 512:515], [xpad.b], [xcarry.b])
                P.ts("dve", xc.ap, xpad.ap[:, 0:512], cw.ap[:, 0, j:j + 1], pv.ap[:, 0, j:j + 1], ALU.mult, ALU.add,
                     [xpad.b, cw.b, pv.b], [xc.b])
                for k in range(1, 4):
                    P.stt("dve", xc.ap, xpad.ap[:, k:k + 512], cw.ap[:, k, j:j + 1], xc.ap, ALU.mult, ALU.add,
                          [xpad.b, cw.b, xc.b], [xc.b])
                P.cp("pool", xcb.ap, xc.ap, [xc.b], [xcb.b])
                P.mm(pg0.ap, w_r.ap[:, j, :], xcb.ap, True, True, [w_r.b, xcb.b], [pg0.b])
                P.mm(pg1.ap, w_i.ap[:, j, :], xcb.ap, True, True, [w_i.b, xcb.b], [pg1.b])
                P.act(r_.ap, pg0.ap, AF.Sigmoid, R=[pg0.b, pv.b], W=[r_.b], bias=pv.ap[:, 1, j:j + 1])
                P.act(ig.ap, pg1.ap, AF.Sigmoid, R=[pg1.b, pv.b], W=[ig.b], bias=pv.ap[:, 2, j:j + 1])
                P.act(a_.ap, r_.ap, AF.Exp, R=[r_.b, clam.b], W=[a_.b], scale=clam.ap[:, j:j + 1])
                P.tt("pool", m_.ap, a_.ap, a_.ap, ALU.mult, [a_.b], [m_.b])
                P.act(m_.ap, m_.ap, AF.Sqrt, R=[m_.b, c["one"].b], W=[m_.b], scale=-1.0, bias=c["one"].ap)
                P.tt("pool", u_.ap, ig.ap, xc.ap, ALU.mult, [ig.b, xc.b], [u_.b])
                P.tt("dve", u_.ap, u_.ap, m_.ap, ALU.mult, [u_.b, m_.b], [u_.b])
                P.op("dve", lambda e, hs=hs, a_=a_, u_=u_, j=j: e.tensor_tensor_scan(
                    out=hs.ap, data0=a_.ap, data1=u_.ap, initial=hprev.ap[:, j:j + 1], op0=ALU.mult, op1=ALU.add),
                    [a_.b, u_.b, hprev.b], [hs.b])
                P.cp("dve", hprev.ap[:, j:j + 1], hs.ap[:, 511:512], [hs.b], [hprev.b])
                P.tt("dve", gT.ap[:, j, :], gy.ap, hs.ap, ALU.mult, [gy.b, hs.b], [gT.b])
                pj += 1
            for tt in range(4):
                for nh in range(2):
                    po = psO[nh]
                    for j in range(8):
                        P.mm(po.ap, gT.ap[:, j, tt * 128:(tt + 1) * 128], w_out.ap[:, j, nh * 512:(nh + 1) * 512],
                             j == 0, j == 7, [gT.b, w_out.b], [po.b])
                    P.tt("dve", ht.ap[:, tt, nh * 512:(nh + 1) * 512], po.ap, ht.ap[:, tt, nh * 512:(nh + 1) * 512],
                         ALU.add, [po.b, ht.b], [ht.b])
            P.dma("sp", hout.ap[r0:r0 + 512, :].rearrange("(tt p) d -> p tt d", p=128), ht.ap,
                  R=[ht.b], W=hout.bufs[ti0:ti0 + 4])
            it += 1


NEG = -1.0e30


def stage_peer(P, I, L, hin, hout, XTd, Gd):
    for s_ in range(BPC):
        peer_route(P, I, L, hin, XTd, Gd, s_)
        peer_main(P, I, L, hin, hout, XTd, Gd, s_)


def peer_route(P, I, L, hin, XTd, Gd, s_):
    P.stage_begin()
    c = load_consts(P, I["consts"])
    identf = P.tile([128, 128], F32)
    P.dma("sp", identf.ap, I["consts"][:, 0:128], W=[identf.b])
    g_bc = P.tile([128, 1024], F32)
    P.dma("sp", g_bc.ap, bcast_rows(I["norm_ffn"][L], D), W=[g_bc.b])
    w_q = P.tile([128, 8, 2048], BF16)
    P.dma("pool", w_q.ap, I["peer_w_q"][L].rearrange("(kc p) n -> p kc n", p=128), W=[w_q.b])
    KT = P.tile([128, 16, 128], BF16)
    ps0, ps1 = P.psb[0], P.psb[1]
    psAB = [P.psb[2], P.psb[3], P.psb[4], P.psb[5]]
    psG = [P.psb[6], P.psb[7]]
    Gq = P.tiles(2, [128, 32, 128], BF16)
    Aq = P.tiles(1, [128, 128, 32], BF16)
    Apq = P.tiles(2, [128, 32, 128], BF16)
    Bg = P.tiles(2, [128, 128, 32], BF16)
    Bp = P.tile([128, 128, 128], BF16)
    Tt = P.tile([128, 8, 16, 32], F32)
    Ksb = P.tile([128, 16, 128], BF16)
    P.dma("pool", Ksb.ap, I["peer_sub_keys"][L].rearrange("h p n d -> n (h p) d"), W=[Ksb.b])
    for half in range(2):
        pv = ps0.ap.bitcast(BF16)
        for q in range(8):
            hp = half * 8 + q
            P.tr(pv[:, q * 128:(q + 1) * 128], Ksb.ap[:, hp, :], c["ident"].ap, [Ksb.b, c["ident"].b], [ps0.b])
        P.cp("act", KT.ap[:, half * 8:(half + 1) * 8, :], pv.rearrange("p (a b) -> p a b", a=8), [ps0.b], [KT.b])
    hts = P.tiles(2, [128, 1024], F32)
    hn = P.tile([128, 1024], BF16)
    junk = P.tile([128, 1024], BF16)
    ss = P.tile([128, 2], F32)
    rstd = P.tile([128, 2], F32)
    XTt = P.tile([128, 8, 128], BF16)
    qT = P.tile([128, 16, 128], BF16)
    s_sb = P.tile([128, 16, 128], F32)
    sv = P.tile([128, 16, 16], F32)
    scr = P.tiles(2, [128, 256], F32)
    cand = P.tile([128, 8, 16, 16], F32)
    cv = P.tile([128, 8, 16], F32)
    negm = P.tile([128, 8], F32)
    negm1 = P.tile([128, 8], F32)
    negm2 = P.tile([128, 8], F32)
    Z = P.tile([128, 8], F32)
    rZ = P.tile([128, 8], F32)
    jk16 = P.tile([128, 16], F32)
    cc = P.tile([128, 8, 16], F32)
    e2 = P.tile([128, 8, 128], F32)
    cT = P.tile([128, 128], F32)
    sv4 = sv.ap.rearrange("p (h two) a -> p h two a", two=2)
    s4 = s_sb.ap.rearrange("p (h two) n -> p h two n", two=2)
    ev = 0
    for tl_ in range(16):
        ti = s_ * 16 + tl_
        r0 = ti * 128
        ht = hts[tl_ % 2]
        P.dma("sp", ht.ap, hin.ap[r0:r0 + 128, :], R=[hin.bufs[ti]], W=[ht.b])
        rms_rstd(P, c, ht.ap, ht.b, junk, ss, rstd, 0)
        P.stt("dve", hn.ap, ht.ap, rstd.ap[:, 0:1], g_bc.ap, ALU.mult, ALU.mult, [ht.b, rstd.b, g_bc.b], [hn.b])
        transpose_tile(P, c, hn, XTt.ap, XTt.b, 8, ps0, "act")
        P.dma("sp", XTd.ap[:, :, r0:r0 + 128], XTt.ap, R=[XTt.b], W=[XTd.bufs[ti]])
        for g in range(4):
            pq = ps1
            for u in range(4):
                hp = g * 4 + u
                for kc in range(8):
                    P.mm(pq.ap[:, u * 128:(u + 1) * 128], w_q.ap[:, kc, hp * 128:(hp + 1) * 128], XTt.ap[:, kc, :],
                         kc == 0, kc == 7, [w_q.b, XTt.b], [pq.b])
            P.cp("act", qT.ap[:, g * 4:(g + 1) * 4, :], pq.ap.rearrange("p (a b) -> p a b", a=4), [pq.b], [qT.b])
        for g in range(4):
            pq = ps0
            for u in range(4):
                hp = g * 4 + u
                P.mm(pq.ap[:, u * 128:(u + 1) * 128], qT.ap[:, hp, :], KT.ap[:, hp, :], True, True, [qT.b, KT.b], [pq.b])
            P.cp("dve", s_sb.ap[:, g * 4:(g + 1) * 4, :], pq.ap.rearrange("p (a b) -> p a b", a=4), [pq.b], [s_sb.b])
        for hp in range(16):
            sc = scr[hp % 2]
            P.op("dve", lambda e, hp=hp: e.max(out=sv.ap[:, hp, 0:8], in_=s_sb.ap[:, hp, :]), [s_sb.b], [sv.b])
            P.op("dve", lambda e, hp=hp, sc=sc: e.match_replace(out=sc.ap[:, 0:128], in_to_replace=sv.ap[:, hp, 0:8],
                                                               in_values=s_sb.ap[:, hp, :], imm_value=NEG),
                 [s_sb.b, sv.b], [sc.b])
            P.op("dve", lambda e, hp=hp, sc=sc: e.max(out=sv.ap[:, hp, 8:16], in_=sc.ap[:, 0:128]), [sc.b], [sv.b])
        P.tt("dve", cand.ap, sv4[:, :, 0, :].unsqueeze(3).to_broadcast([128, 8, 16, 16]),
             sv4[:, :, 1, :].unsqueeze(2).to_broadcast([128, 8, 16, 16]), ALU.add, [sv.b], [cand.b])
        for h in range(8):
            sc = scr[h % 2]
            cflat = cand.ap[:, h, :, :].rearrange("p a b -> p (a b)")
            P.op("dve", lambda e, h=h, cflat=cflat: e.max(out=cv.ap[:, h, 0:8], in_=cflat), [cand.b], [cv.b])
            P.op("dve", lambda e, h=h, sc=sc, cflat=cflat: e.match_replace(out=sc.ap, in_to_replace=cv.ap[:, h, 0:8],
                                                                          in_values=cflat, imm_value=NEG),
                 [cand.b, cv.b], [sc.b])
            P.op("dve", lambda e, h=h, sc=sc: e.max(out=cv.ap[:, h, 8:16], in_=sc.ap), [sc.b], [cv.b])
        P.ts("dve", negm1.ap, sv4[:, :, 0, 0], -1.0, None, ALU.mult, None, [sv.b], [negm1.b])
        P.ts("dve", negm2.ap, sv4[:, :, 1, 0], -1.0, None, ALU.mult, None, [sv.b], [negm2.b])
        P.tt("dve", negm.ap, negm1.ap, negm2.ap, ALU.add, [negm1.b, negm2.b], [negm.b])
        for h in range(8):
            P.act(jk16.ap, cv.ap[:, h, :], AF.Exp, R=[cv.b, negm.b], W=[jk16.b, Z.b], bias=negm.ap[:, h:h + 1],
                  accum_out=Z.ap[:, h:h + 1])
        P.op("dve", lambda e: e.reciprocal(out=rZ.ap, in_=Z.ap), [Z.b], [rZ.b])
        for h in range(8):
            P.act(cc.ap[:, h, :], sv.ap[:, 2 * h, :], AF.Exp, R=[sv.b, negm1.b], W=[cc.b], bias=negm1.ap[:, h:h + 1])
        for h in range(8):
            P.act(e2.ap[:, h, :], s_sb.ap[:, 2 * h + 1, :], AF.Exp, R=[s_sb.b, negm2.b], W=[e2.b], bias=negm2.ap[:, h:h + 1])
        P.tt("dve", cc.ap, cc.ap, rZ.ap.unsqueeze(2).to_broadcast([128, 8, 16]), ALU.mult, [cc.b, rZ.b], [cc.b])
        P.tr(ps1.ap[:, 0:128], cc.ap.rearrange("p h a -> p (h a)"), identf.ap, [cc.b, identf.b], [ps1.b])
        P.cp("act", cT.ap, ps1.ap[:, 0:128], [ps1.b], [cT.b])
        nb = 0
        for g in range(4):
            bg = Bg[g % 2]
            i0 = g * 32
            P.tt("pool", Tt.ap, s4[:, :, 1, i0:i0 + 32].unsqueeze(2).to_broadcast([128, 8, 16, 32]),
                 sv4[:, :, 0, :].unsqueeze(3).to_broadcast([128, 8, 16, 32]), ALU.add, [s_sb.b, sv.b], [Tt.b])
            P.tt("dve", Tt.ap, Tt.ap, cv.ap[:, :, 15:16].unsqueeze(3).to_broadcast([128, 8, 16, 32]), ALU.is_ge,
                 [Tt.b, cv.b], [Tt.b])
            P.tt("pool", bg.ap.rearrange("p (h a) i -> p h a i", h=8), Tt.ap,
                 e2.ap[:, :, i0:i0 + 32].unsqueeze(2).to_broadcast([128, 8, 16, 32]), ALU.mult, [Tt.b, e2.b], [bg.b])
            for blk in range(4):
                pb = psAB[nb % 4]
                nb += 1
                pbv = pb.ap.bitcast(BF16)
                for ii in range(8):
                    P.tr(pbv[:, ii * 128:(ii + 1) * 128], bg.ap[:, :, blk * 8 + ii], c["ident"].ap,
                         [bg.b, c["ident"].b], [pb.b])
                P.cp("act" if ev % 2 == 0 else "dve", Bp.ap[:, i0 + blk * 8:i0 + blk * 8 + 8, :],
                     pbv.rearrange("p (a b) -> p a b", a=8), [pb.b], [Bp.b])
                ev += 1
        for q4 in range(4):
            aq = Aq[0]
            apq = Apq[q4 % 2]
            gq = Gq[q4 % 2]
            i0 = q4 * 32
            P.tt("dve", aq.ap.rearrange("p (h a) i -> p h a i", h=8),
                 s4[:, :, 0, i0:i0 + 32].unsqueeze(2).to_broadcast([128, 8, 16, 32]),
                 sv4[:, :, 0, :].unsqueeze(3).to_broadcast([128, 8, 16, 32]), ALU.is_equal, [s_sb.b, sv.b], [aq.b])
            for blk in range(4):
                pb = psAB[nb % 4]
                nb += 1
                pbv = pb.ap.bitcast(BF16)
                for ii in range(8):
                    P.tr(pbv[:, ii * 128:(ii + 1) * 128], aq.ap[:, :, blk * 8 + ii], c["ident"].ap,
                         [aq.b, c["ident"].b], [pb.b])
                P.tt("dve", apq.ap[:, blk * 8:blk * 8 + 8, :], pbv.rearrange("p (a b) -> p a b", a=8),
                     cT.ap.unsqueeze(1).to_broadcast([128, 8, 128]), ALU.mult, [pb.b, cT.b], [apq.b])
            for tg in range(8):
                pg = psG[tg % 2]
                for tt in range(16):
                    t = tg * 16 + tt
                    P.mm(pg.ap[:, tt * 32:(tt + 1) * 32], Bp.ap[:, :, t], apq.ap[:, :, t], True, True,
                         [Bp.b, apq.b], [pg.b])
                P.cp("act" if ev % 2 == 0 else "dve", gq.ap[:, :, tg * 16:(tg + 1) * 16],
                     pg.ap.rearrange("p (t i) -> p i t", t=16), [pg.b], [gq.b])
                ev += 1
            for hh in range(2):
                a0 = i0 + hh * 16
                P.dma("sp", Gd.ap[ti, a0:a0 + 16].rearrange("a b t -> b a t"), gq.ap[:, hh * 16:(hh + 1) * 16, :],
                      R=[gq.b], W=[Gd.bufs[ti]])


def peer_main(P, I, L, hin, hout, XTd, Gd, s_):
    P.stage_begin()
    c = load_consts(P, I["consts"])
    XT = P.tile([128, 8, S], BF16)
    t0 = s_ * 16
    P.dma("sp", XT.ap, XTd.ap[:, :, s_ * S:(s_ + 1) * S], R=XTd.bufs[t0:t0 + 16], W=[XT.b])
    oacc = P.tile([128, 16, 1024], F32)
    oaccb = [Buf() for _ in range(16)]
    NB = 4
    Ust = P.tiles(2, [128, 1024], BF16)
    UT = [P.tiles(NB, [128, 8, 128], BF16) for _ in range(2)]
    Vb = [P.tiles(NB, [128, 1024], BF16) for _ in range(2)]
    Gb = [P.tiles(NB, [128, 16, 128], BF16) for _ in range(2)]
    actb = P.tiles(3, [128, 256], BF16)
    Wb = P.tiles(3, [128, 256], BF16)
    psS = [P.psb[0], P.psb[1]]
    psT = P.psb[2]
    psO = [P.psb[4], P.psb[5], P.psb[6], P.psb[7]]
    ngrp = 128 // NB

    def load_group(grp):
        par = grp % 2
        for bi in range(NB):
            i1 = grp * NB + bi
            ust = Ust[bi % 2]
            P.dma("pool", ust.ap, I["peer_u"][L, i1 * 128:(i1 + 1) * 128, :], W=[ust.b])
            P.dma("pool", Vb[par][bi].ap, I["peer_v"][L, i1 * 128:(i1 + 1) * 128, :], W=[Vb[par][bi].b])
            P.dma("sp", Gb[par][bi].ap, Gd.ap[t0:t0 + 16, i1].rearrange("n b t -> b n t"),
                  R=Gd.bufs[t0:t0 + 16], W=[Gb[par][bi].b])
            pv = psT.ap.bitcast(BF16)
            for dc in range(8):
                P.tr(pv[:, dc * 128:(dc + 1) * 128], ust.ap[:, dc * 128:(dc + 1) * 128], c["ident"].ap,
                     [ust.b, c["ident"].b], [psT.b])
            P.cp("act", UT[par][bi].ap, pv.rearrange("p (a b) -> p a b", a=8), [psT.b], [UT[par][bi].b])

    load_group(0)
    wi = 0
    for grp in range(ngrp):
        par = grp % 2
        if grp + 1 < ngrp:
            load_group(grp + 1)
        items = [(pair, bi) for pair in range(8) for bi in range(NB)]
        pend = None

        def emit_S(pair, bi, k):
            ps = psS[k % 2]
            for dc in range(8):
                P.mm(ps.ap[:, 0:256], UT[par][bi].ap[:, dc, :], XT.ap[:, dc, pair * 256:(pair + 1) * 256], dc == 0, dc == 7,
                     [UT[par][bi].b, XT.b], [ps.b])
            a_ = actb[k % 3]
            w_ = Wb[k % 3]
            P.act(a_.ap, ps.ap[:, 0:256], AF.Gelu_apprx_tanh, R=[ps.b], W=[a_.b])
            P.tt("pool" if k % 2 == 0 else "dve", w_.ap, a_.ap,
                 Gb[par][bi].ap[:, 2 * pair:2 * pair + 2, :].rearrange("p a b -> p (a b)"), ALU.mult,
                 [a_.b, Gb[par][bi].b], [w_.b])
            return w_

        def emit_O(pair, bi, w_):
            for tt in range(2):
                for nh in range(2):
                    po = psO[tt * 2 + nh]
                    P.mm(po.ap, w_.ap[:, tt * 128:(tt + 1) * 128], Vb[par][bi].ap[:, nh * 512:(nh + 1) * 512],
                         bi == 0, bi == NB - 1, [w_.b, Vb[par][bi].b], [po.b])
            if bi == NB - 1:
                for tt in range(2):
                    tix = 2 * pair + tt
                    for nh in range(2):
                        po = psO[tt * 2 + nh]
                        dst = oacc.ap[:, tix, nh * 512:(nh + 1) * 512]
                        if grp == 0:
                            P.cp("dve", dst, po.ap, [po.b], [oaccb[tix]])
                        else:
                            P.tt("dve", dst, po.ap, dst, ALU.add, [po.b, oaccb[tix]], [oaccb[tix]])

        for (pair, bi) in items:
            w_ = emit_S(pair, bi, wi)
            wi += 1
            if pend is not None:
                emit_O(*pend)
            pend = (pair, bi, w_)
        emit_O(*pend)
    hts = P.tiles(2, [128, 1024], F32)
    for tl_ in range(16):
        ti = t0 + tl_
        ht = hts[tl_ % 2]
        P.dma("sp", ht.ap, hin.ap[ti * 128:(ti + 1) * 128, :], R=[hin.bufs[ti]], W=[ht.b])
        P.tt("dve", ht.ap, ht.ap, oacc.ap[:, tl_, :], ALU.add, [ht.b, oaccb[tl_]], [ht.b])
        P.dma("sp", hout.ap[ti * 128:(ti + 1) * 128, :], ht.ap, R=[ht.b], W=[hout.bufs[ti]])


def stage_ple(P, I, L, hin, hout, final):
    P.stage_begin()
    c = load_consts(P, I["consts"])
    w_g = P.tile([128, 8, 1024], BF16)
    P.dma("pool", w_g.ap, I["ple_w_gate"][L].rearrange("(kc p) n -> p kc n", p=128), W=[w_g.b])
    w_p = P.tile([128, 2, 1024], BF16)
    P.dma("pool", w_p.ap, I["ple_w_proj"][L].rearrange("(kc p) n -> p kc n", p=128), W=[w_p.b])
    g_bc = P.tile([128, 1024], F32)
    P.dma("sp", g_bc.ap, bcast_rows(I["norm_ple"][L], D), W=[g_bc.b])
    gf_bc = P.tile([128, 1024], F32)
    P.dma("sp", gf_bc.ap, bcast_rows(I["final_norm"], D), W=[gf_bc.b])
    pflat = I["p"][L].rearrange("b s d -> (b s) d")
    hts = P.tiles(3, [128, 1024], F32)
    pts = P.tiles(2, [128, 256], F32)
    hns = P.tiles(2, [128, 1024], BF16)
    pbs = P.tiles(2, [128, 256], BF16)
    hnTs = P.tiles(2, [128, 8, 128], BF16)
    pTs = P.tiles(2, [128, 2, 128], BF16)
    sgs = P.tiles(2, [128, 512], F32)
    junk = P.tile([128, 1024], BF16)
    ss = P.tile([128, 2], F32)
    rstd = P.tile([128, 2], F32)
    outs = P.tiles(2, [128, 1024], F32)
    for ti in range(NT):
        ht = hts[ti % 3]
        pt = pts[ti % 2]
        hn = hns[ti % 2]
        pb = pbs[ti % 2]
        hnT = hnTs[ti % 2]
        pT = pTs[ti % 2]
        r0 = ti * 128
        P.dma("sp", ht.ap, hin.ap[r0:r0 + 128, :], R=[hin.bufs[ti]], W=[ht.b])
        P.dma("sp", pt.ap, pflat[r0:r0 + 128, :], W=[pt.b])
        rms_rstd(P, c, ht.ap, ht.b, junk, ss, rstd, 0)
        P.stt("dve", hn.ap, ht.ap, rstd.ap[:, 0:1], g_bc.ap, ALU.mult, ALU.mult, [ht.b, rstd.b, g_bc.b], [hn.b])
        transpose_tile(P, c, hn, hnT.ap, hnT.b, 8, P.psb[0], "act")
        P.cp("pool", pb.ap, pt.ap, [pt.b], [pb.b])
        transpose_tile(P, c, pb, pT.ap, pT.b, 2, P.psb[1], "act")
        for nh in range(2):
            pg = P.psb[2 + nh]
            pp = P.psb[4 + nh]
            sg = sgs[nh]
            for kc in range(8):
                P.mm(pg.ap, hnT.ap[:, kc, :], w_g.ap[:, kc, nh * 512:(nh + 1) * 512], kc == 0, kc == 7, [hnT.b, w_g.b], [pg.b])
            for kc in range(2):
                P.mm(pp.ap, pT.ap[:, kc, :], w_p.ap[:, kc, nh * 512:(nh + 1) * 512], kc == 0, kc == 1, [pT.b, w_p.b], [pp.b])
            P.act(sg.ap, pg.ap, AF.Sigmoid, R=[pg.b], W=[sg.b])
            P.tt("dve", sg.ap, sg.ap, pp.ap, ALU.mult, [sg.b, pp.b], [sg.b])
            P.tt("dve", ht.ap[:, nh * 512:(nh + 1) * 512], ht.ap[:, nh * 512:(nh + 1) * 512], sg.ap, ALU.add, [ht.b, sg.b], [ht.b])
        if final:
            ot = outs[ti % 2]
            rms_rstd(P, c, ht.ap, ht.b, junk, ss, rstd, 1)
            P.stt("dve", ot.ap, ht.ap, rstd.ap[:, 1:2], gf_bc.ap, ALU.mult, ALU.mult, [ht.b, rstd.b, gf_bc.b], [ot.b])
            P.dma("sp", hout.ap[r0:r0 + 128, :], ot.ap, R=[ot.b], W=[hout.bufs[ti]])
        else:
            P.dma("sp", hout.ap[r0:r0 + 128, :], ht.ap, R=[ht.b], W=[hout.bufs[ti]])


def stage_attn(P, I, hin, hout):
    for s_ in range(BPC):
        attn_seq(P, I, hin, hout, s_)


def attn_seq(P, I, hin, hout, s_):
    P.stage_begin()
    c = load_consts(P, I["consts"])
    negtri = P.tile([128, 128], BF16)
    P.dma("pool", negtri.ap, I["consts"][:, 128:256], W=[negtri.b])
    negones = P.tile([128, 128], BF16)
    P.dma("pool", negones.ap, I["consts"][:, 256:384], W=[negones.b])
    wm = P.tile([128, 896], BF16)
    P.dma("pool", wm.ap, I["consts"][:, 384:384 + 896], W=[wm.b])
    qT = P.tile([128, 8, S], BF16)
    kT = P.tile([128, 8, S], BF16)
    v = P.tile([128, 16, 1024], BF16)
    qb = [[[Buf(), Buf()] for _ in range(4)] for _ in range(8)]
    mark = P.aoff
    w_q = P.tile([128, 8, 1024], BF16)
    P.dma("pool", w_q.ap, I["b_w_q"][0].rearrange("(kc p) n -> p kc n", p=128), W=[w_q.b])
    w_kv = P.tile([128, 8, 2048], BF16)
    P.dma("pool", w_kv.ap, I["w_kv"].rearrange("(kc p) n -> p kc n", p=128), W=[w_kv.b])
    g1 = P.tile([128, 1024], F32)
    P.dma("sp", g1.ap, bcast_rows(I["norm_mix"][1], D), W=[g1.b])
    g2 = P.tile([128, 1024], F32)
    P.dma("sp", g2.ap, bcast_rows(I["kv_norm"], D), W=[g2.b])
    ht = P.tile([128, 4, 1024], F32)
    hn = P.tile([128, 1024], BF16)
    hkv = P.tile([128, 1024], BF16)
    hnT = P.tile([128, 8, 512], BF16)
    hkvT = P.tile([128, 8, 512], BF16)
    junk = P.tile([128, 1024], BF16)
    ss = P.tile([128, 4], F32)
    rstd = P.tile([128, 4], F32)
    for ch in range(4):
        r0 = s_ * S + ch * 512
        ti0 = r0 // 128
        P.dma("sp", ht.ap, hin.ap[r0:r0 + 512, :].rearrange("(tt p) d -> p tt d", p=128), R=hin.bufs[ti0:ti0 + 4], W=[ht.b])
        for tt in range(4):
            rms_rstd(P, c, ht.ap[:, tt, :], ht.b, junk, ss, rstd, tt)
            P.stt("dve", hn.ap, ht.ap[:, tt, :], rstd.ap[:, tt:tt + 1], g1.ap, ALU.mult, ALU.mult, [ht.b, rstd.b, g1.b], [hn.b])
            P.stt("dve", hkv.ap, ht.ap[:, tt, :], rstd.ap[:, tt:tt + 1], g2.ap, ALU.mult, ALU.mult, [ht.b, rstd.b, g2.b], [hkv.b])
            transpose_tile(P, c, hn, hnT.ap[:, :, tt * 128:(tt + 1) * 128], hnT.b, 8, P.psb[4], "act")
            transpose_tile(P, c, hkv, hkvT.ap[:, :, tt * 128:(tt + 1) * 128], hkvT.b, 8, P.psb[5], "act")
        for jc in range(8):
            pq = P.psb[6]
            for kc in range(8):
                P.mm(pq.ap, w_q.ap[:, kc, jc * 128:(jc + 1) * 128], hnT.ap[:, kc, :], kc == 0, kc == 7, [w_q.b, hnT.b], [pq.b])
            P.act(qT.ap[:, jc, ch * 512:(ch + 1) * 512], pq.ap, AF.Copy, R=[pq.b], W=qb[jc][ch], scale=0.125)
            pk = P.psb[7]
            for kc in range(8):
                P.mm(pk.ap, w_kv.ap[:, kc, jc * 128:(jc + 1) * 128], hkvT.ap[:, kc, :], kc == 0, kc == 7, [w_kv.b, hkvT.b], [pk.b])
            P.cp("dve", kT.ap[:, jc, ch * 512:(ch + 1) * 512], pk.ap, [pk.b], [kT.b])
        for tt in range(4):
            for nh in range(2):
                pv_ = P.psb[6 + nh]
                for kc in range(8):
                    P.mm(pv_.ap, hkvT.ap[:, kc, tt * 128:(tt + 1) * 128], w_kv.ap[:, kc, 1024 + nh * 512:1024 + (nh + 1) * 512],
                         kc == 0, kc == 7, [hkvT.b, w_kv.b], [pv_.b])
                P.cp("act" if nh == 0 else "dve", v.ap[:, ch * 4 + tt, nh * 512:(nh + 1) * 512], pv_.ap, [pv_.b], [v.b])
    P.barrier()
    P.aoff = mark
    w_o = P.tile([128, 8, 1024], BF16)
    P.dma("pool", w_o.ap, I["b_w_o"][0].rearrange("(kc p) n -> p kc n", p=128), W=[w_o.b])
    es = P.tiles(2, [128, 512], F32)
    sps = P.tiles(3, [128, 512], BF16)
    As = P.tiles(3, [128, 512], BF16)
    Srun = P.tile([128, 512], BF16)
    psZs = [P.psb[0], P.psb[1]]
    psOs = [P.psb[2], P.psb[3]]
    n = 0
    no = 0
    for h in range(16):
        jc = h // 2
        par = h % 2
        po = par * 64
        for cq in range(4):
            top = 4 * cq + 3
            pso = psOs[no % 2]
            no += 1
            for kb in range(top, -1, -1):
                psZ = psZs[n % 2]
                e_ = es[n % 2]
                sp = sps[n % 3]
                A_ = As[n % 3]
                n += 1
                P.mm(psZ.ap, kT.ap[po:po + 64, jc, kb * 128:(kb + 1) * 128], qT.ap[po:po + 64, jc, cq * 512:(cq + 1) * 512],
                     True, False, [kT.b, qb[jc][cq][par]], [psZ.b])
                P.act(e_.ap, psZ.ap, AF.Exp, R=[psZ.b], W=[e_.b])
                P.act(sp.ap, e_.ap, AF.Ln, R=[e_.b, c["one"].b], W=[sp.b], bias=c["one"].ap)
                diag = kb >= 4 * cq
                if diag:
                    dl = cq * 512 - kb * 128
                    mask = wm.ap[:, 384 + dl:384 + dl + 512]
                    P.tt("dve", sp.ap, sp.ap, mask, ALU.mult, [sp.b, wm.b], [sp.b])
                P.mm(psZ.ap, negtri.ap, sp.ap, False, kb == top, [negtri.b, sp.b], [psZ.b])
                if kb != top:
                    P.mm(psZ.ap, negones.ap, Srun.ap, False, True, [negones.b, Srun.b], [psZ.b])
                if kb == top:
                    P.cp("pool", Srun.ap, sp.ap, [sp.b], [Srun.b])
                elif kb > 0:
                    P.tt("pool", Srun.ap, Srun.ap, sp.ap, ALU.add, [Srun.b, sp.b], [Srun.b])
                P.act(A_.ap, psZ.ap, AF.Exp, R=[psZ.b], W=[A_.b])
                if diag:
                    P.tt("dve", A_.ap, A_.ap, mask, ALU.mult, [A_.b, wm.b], [A_.b])
                P.mm(pso.ap[po:po + 64, :], v.ap[:, kb, h * 64:(h + 1) * 64], A_.ap, kb == top, kb == 0, [v.b, A_.b], [pso.b])
            P.cp("dve", qT.ap[po:po + 64, jc, cq * 512:(cq + 1) * 512], pso.ap[po:po + 64, :], [pso.b], [qb[jc][cq][par]])
    hts = P.tiles(2, [128, 1024], F32)
    for tl_ in range(16):
        ti = s_ * 16 + tl_
        cq = tl_ // 4
        ht2 = hts[tl_ % 2]
        P.dma("sp", ht2.ap, hin.ap[ti * 128:(ti + 1) * 128, :], R=[hin.bufs[ti]], W=[ht2.b])
        for nh in range(2):
            po_ = P.psb[4 + nh]
            for jc in range(8):
                P.mm(po_.ap, qT.ap[:, jc, tl_ * 128:(tl_ + 1) * 128], w_o.ap[:, jc, nh * 512:(nh + 1) * 512], jc == 0, jc == 7,
                     [qb[jc][cq][0], qb[jc][cq][1], w_o.b], [po_.b])
            P.tt("dve", ht2.ap[:, nh * 512:(nh + 1) * 512], ht2.ap[:, nh * 512:(nh + 1) * 512], po_.ap, ALU.add, [ht2.b, po_.b], [ht2.b])
        P.dma("sp", hout.ap[ti * 128:(ti + 1) * 128, :], ht2.ap, R=[ht2.b], W=[hout.bufs[ti]])


WEIGHT_SPECS = [
    ("norm_mix", [2, 1024]), ("a_w_in", [1, 1024, 2048]), ("a_conv_w", [1, 4, 1024]), ("a_conv_b", [1, 1024]),
    ("a_w_r", [1, 8, 128, 128]), ("a_w_i", [1, 8, 128, 128]), ("a_b_r", [1, 1024]), ("a_b_i", [1, 1024]),
    ("a_lambda", [1, 1024]), ("a_w_out", [1, 1024, 1024]), ("kv_norm", [1024]), ("w_kv", [1024, 2048]),
    ("b_w_q", [1, 1024, 1024]), ("b_w_o", [1, 1024, 1024]), ("norm_ffn", [2, 1024]), ("peer_w_q", [2, 1024, 2048]),
    ("peer_sub_keys", [2, 8, 2, 128, 128]), ("peer_u", [2, 16384, 1024]), ("peer_v", [2, 16384, 1024]),
    ("norm_ple", [2, 1024]), ("ple_w_gate", [2, 1024, 1024]), ("ple_w_proj", [2, 256, 1024]), ("final_norm", [1024]),
]
NCONST = 128 * 3 + 896


def make_consts():
    cst = np.zeros((128, NCONST), np.float32)
    cst[:, 0:128] = np.eye(128, dtype=np.float32)
    j = np.arange(128)[:, None]
    s_ = np.arange(128)[None, :]
    cst[:, 128:256] = -(j >= s_).astype(np.float32)
    cst[:, 256:384] = -1.0
    xx = np.arange(896)[None, :]
    cst[:, 384:384 + 896] = (j < xx - 384).astype(np.float32)
    return cst


def build(stages):
    nc = bass.Bass("TRN2", target_bir_lowering=False)
    I = {}
    I["x"] = nc.dram_tensor("x", [BPC, S, D], F32, kind="ExternalInput").ap()
    I["p"] = nc.dram_tensor("p", [2, BPC, S, PLE], F32, kind="ExternalInput").ap()
    for nm, shp in WEIGHT_SPECS:
        I[nm] = nc.dram_tensor(nm, shp, F32, kind="ExternalInput").ap()
    I["consts"] = nc.dram_tensor("consts", [128, NCONST], F32, kind="ExternalInput").ap()
    out = nc.dram_tensor("out", [T, D], F32, kind="ExternalOutput").ap()
    P = Prog(nc)
    hx = Dram(I["x"].rearrange("b s d -> (b s) d"), NT)
    hs = [Dram(nc.dram_tensor("hscr%d" % i, [T, D], F32, kind="Internal").ap(), NT) for i in range(2)]
    XTd = Dram(nc.dram_tensor("xtd", [128, 8, T], BF16, kind="Internal").ap(), NT)
    Gd = Dram(nc.dram_tensor("gd", [NT, 128, 128, 128], BF16, kind="Internal").ap(), NT)
    hout = Dram(out, NT)
    cur = hx
    nst = len(stages)
    for si, st in enumerate(stages):
        dst = hout if si == nst - 1 else hs[si % 2]
        if st == "rglru":
            stage_rglru(P, I, cur, dst)
        elif st in ("peer0", "peer1"):
            stage_peer(P, I, int(st[-1]), cur, dst, XTd, Gd)
        elif st in ("ple0", "ple1"):
            stage_ple(P, I, int(st[-1]), cur, dst, st == "ple1")
        elif st == "attn":
            stage_attn(P, I, cur, dst)
        else:
            raise ValueError(st)
        cur = dst
    P.barrier()
    return nc, P


ALL_STAGES = ["rglru", "peer0", "ple0", "attn", "peer1", "ple1"]


def kernel(**inputs):
    nc, P = build(ALL_STAGES)
    cst = make_consts()
    in_maps = []
    for c in range(NCORES):
        m = {"consts": cst}
        m["x"] = np.ascontiguousarray(inputs["x"][c * BPC:(c + 1) * BPC])
        m["p"] = np.ascontiguousarray(inputs["p"][:, c * BPC:(c + 1) * BPC])
        for nm, _ in WEIGHT_SPECS:
            m[nm] = np.ascontiguousarray(inputs[nm])
        in_maps.append(m)
    res = run_bass_kernel_spmd(nc, in_maps, core_ids=list(range(NCORES)))
    outs = [np.asarray(r["out"]).reshape(BPC, S, D) for r in res.results]
    return np.concatenate(outs, axis=0).astype(np.float32)
```

```python
import numpy as np
import concourse.bass as bass
import concourse.mybir as mybir
from concourse.bass import AP
from concourse.bass_utils import run_bass_kernel_spmd

F32 = mybir.dt.float32
BF16 = mybir.dt.bfloat16
U32 = mybir.dt.uint32
AF = mybir.ActivationFunctionType
ALU = mybir.AluOpType
AX = mybir.AxisListType

NCORES = 8
BPC = 2
S = 2048
D = 1024
T = BPC * S
NT = T // 128
PLE = 256
NKEY = 128
ARENA_WORDS = 52000
KDMA = 12
EPS = 1e-6
DSIZE = {F32: 4, BF16: 2, U32: 4}


class Buf:
    __slots__ = ("w", "r")

    def __init__(self):
        self.w = {}
        self.r = {}


class Tl:
    def __init__(self, ap, b=None):
        self.ap = ap
        self.b = b if b is not None else Buf()


class Dram:
    def __init__(self, ap, ntiles):
        self.ap = ap
        self.bufs = [Buf() for _ in range(ntiles)]


class Prog:
    CE = ("pe", "act", "dve", "pool")
    DQ = ("sp", "pool", "act")

    def __init__(self, nc):
        self.nc = nc
        self.eng = {"pe": nc.tensor, "act": nc.scalar, "dve": nc.vector, "pool": nc.gpsimd, "sp": nc.sync}
        self.semh = {}
        for e in self.CE:
            self.semh[e] = nc.alloc_semaphore("c_" + e)
        for q in self.DQ:
            for k in range(KDMA):
                self.semh["%s_d%d" % (q, k)] = nc.alloc_semaphore("d_%s_%d" % (q, k))
        self.cnt = {e: 0 for e in self.CE}
        self.dcnt = {q: 0 for q in self.DQ}
        self.seen = {e: {} for e in self.eng}
        self.arena = nc.alloc_sbuf_tensor("arena", [128, ARENA_WORDS], F32).ap()
        self.aoff = 0
        self.psb = [Tl(nc.alloc_psum_tensor("psb%d" % i, [128, 512], F32).ap()) for i in range(8)]
        self.ninst = 0

    def tile(self, shape, dtype=F32):
        n = 1
        for s_ in shape[1:]:
            n *= s_
        nbytes = (n * DSIZE[dtype] + 31) // 32 * 32
        words = nbytes // 4
        off = self.aoff
        self.aoff += words
        assert self.aoff <= ARENA_WORDS, ("arena overflow", self.aoff)
        v = self.arena[0:shape[0], off:off + n * DSIZE[dtype] // 4]
        if dtype != F32:
            v = v.bitcast(dtype)
        if len(shape) == 3:
            v = v.rearrange("p (a b) -> p a b", a=shape[1], b=shape[2])
        elif len(shape) == 4:
            v = v.rearrange("p (a b c) -> p a b c", a=shape[1], b=shape[2], c=shape[3])
        return Tl(v)

    def tiles(self, n, shape, dtype=F32):
        return [self.tile(shape, dtype) for _ in range(n)]

    def _wait(self, eng, sem, val):
        if val <= 0 or self.seen[eng].get(sem, 0) >= val:
            return
        self.seen[eng][sem] = val
        self.eng[eng].wait_ge(self.semh[sem], val)

    def _deps(self, eng, R, W):
        need = {}
        for b in R:
            for s_, v in b.w.items():
                if need.get(s_, 0) < v:
                    need[s_] = v
        for b in W:
            for s_, v in b.w.items():
                if need.get(s_, 0) < v:
                    need[s_] = v
            for s_, v in b.r.items():
                if need.get(s_, 0) < v:
                    need[s_] = v
        for s_, v in need.items():
            if eng == "pe" and s_ == "pe":
                continue
            self._wait(eng, s_, v)

    def _mark(self, sem, val, R, W):
        for b in W:
            b.w = {sem: val}
            b.r = {}
        for b in R:
            if b in W:
                continue
            if b.r.get(sem, 0) < val:
                b.r[sem] = val

    def op(self, eng, fn, R=(), W=()):
        self._deps(eng, R, W)
        self.cnt[eng] += 1
        fn(self.eng[eng]).then_inc(self.semh[eng], 1)
        self._mark(eng, self.cnt[eng], R, W)
        self.ninst += 1

    def dma(self, q, out, in_, R=(), W=(), **kw):
        self._deps(q, R, W)
        i = self.dcnt[q]
        self.dcnt[q] += 1
        k = i % KDMA
        val = 16 * (i // KDMA + 1)
        sem = "%s_d%d" % (q, k)
        self._wait(q, sem, val - 16)
        self.eng[q].dma_start(out=out, in_=in_, **kw).then_inc(self.semh[sem], 16)
        self._mark(sem, val, R, W)
        self.ninst += 1

    def barrier(self):
        for e in self.eng:
            for s_ in self.CE:
                self._wait(e, s_, self.cnt[s_])
            for q in self.DQ:
                for k in range(KDMA):
                    i = self.dcnt[q]
                    n = i // KDMA + (1 if k < i % KDMA else 0)
                    self._wait(e, "%s_d%d" % (q, k), 16 * n)

    def stage_begin(self):
        self.barrier()
        self.aoff = 0

    def mm(self, out, lhsT, rhs, start, stop, R, W):
        self.op("pe", lambda e: e.matmul(out, lhsT=lhsT, rhs=rhs, start=start, stop=stop), R, W)

    def tr(self, out, in_, ident, R, W):
        self.op("pe", lambda e: e.transpose(out, in_, ident), R, W)

    def act(self, out, in_, func, R, W, **kw):
        self.op("act", lambda e: e.activation(out=out, in_=in_, func=func, **kw), R, W)

    def tt(self, eng, out, in0, in1, op, R, W):
        self.op(eng, lambda e: e.tensor_tensor(out=out, in0=in0, in1=in1, op=op), R, W)

    def ts(self, eng, out, in0, s1, s2, op0, op1, R, W):
        if op1 is None:
            self.op(eng, lambda e: e.tensor_scalar(out=out, in0=in0, scalar1=s1, scalar2=None, op0=op0), R, W)
        else:
            self.op(eng, lambda e: e.tensor_scalar(out=out, in0=in0, scalar1=s1, scalar2=s2, op0=op0, op1=op1), R, W)

    def stt(self, eng, out, in0, scalar, in1, op0, op1, R, W):
        self.op(eng, lambda e: e.scalar_tensor_tensor(out=out, in0=in0, scalar=scalar, in1=in1, op0=op0, op1=op1), R, W)

    def cp(self, eng, out, in_, R, W):
        if eng == "act":
            self.op("act", lambda e: e.activation(out=out, in_=in_, func=AF.Copy), R, W)
        else:
            self.op(eng, lambda e: e.tensor_copy(out=out, in_=in_), R, W)

    def memset(self, eng, ap, val, W):
        self.op(eng, lambda e: e.memset(ap, val), (), W)


def bcast_rows(dram_ap_1d, n):
    return AP(dram_ap_1d.tensor, dram_ap_1d.offset, [[0, 128], [1, n]])


class Common:
    pass


def load_consts(P, consts):
    c = {}
    ident = P.tile([128, 128], BF16)
    P.dma("pool", ident.ap, consts[:, 0:128], W=[ident.b])
    c["ident"] = ident
    one = P.tile([128, 1], F32)
    P.memset("dve", one.ap, 1.0, [one.b])
    c["one"] = one
    eps = P.tile([128, 1], F32)
    P.memset("dve", eps.ap, EPS, [eps.b])
    c["eps"] = eps
    return c


def rms_rstd(P, c, x_ap, xb, junk, ss, rstd, col):
    P.act(junk.ap, x_ap, AF.Square, R=[xb], W=[junk.b, ss.b], accum_out=ss.ap[:, col:col + 1])
    P.act(rstd.ap[:, col:col + 1], ss.ap[:, col:col + 1], AF.Sqrt, R=[ss.b, c["eps"].b], W=[rstd.b],
          scale=1.0 / D, bias=c["eps"].ap)
    P.op("dve", lambda e: e.reciprocal(out=rstd.ap[:, col:col + 1], in_=rstd.ap[:, col:col + 1]), [rstd.b], [rstd.b])


def transpose_tile(P, c, src, dst_ap, dstb, nk, psT, evac_eng):
    pv = psT.ap.bitcast(BF16)
    for kc in range(nk):
        P.tr(pv[:, kc * 128:(kc + 1) * 128], src.ap[:, kc * 128:(kc + 1) * 128], c["ident"].ap,
             R=[src.b, c["ident"].b], W=[psT.b])
    P.cp(evac_eng, dst_ap, pv[:, 0:nk * 128].rearrange("p (k t) -> p k t", k=nk), [psT.b], [dstb])


def stage_rglru(P, I, hin, hout):
    nc = P.nc
    P.stage_begin()
    c = load_consts(P, I["consts"])
    w_in = P.tile([128, 8, 2048], BF16)
    P.dma("pool", w_in.ap, I["a_w_in"][0].rearrange("(kc p) n -> p kc n", p=128), W=[w_in.b])
    w_out = P.tile([128, 8, 1024], BF16)
    P.dma("pool", w_out.ap, I["a_w_out"][0].rearrange("(kc p) n -> p kc n", p=128), W=[w_out.b])
    w_r = P.tile([128, 8, 128], BF16)
    P.dma("pool", w_r.ap, I["a_w_r"][0].rearrange("h i j -> i h j"), W=[w_r.b])
    w_i = P.tile([128, 8, 128], BF16)
    P.dma("pool", w_i.ap, I["a_w_i"][0].rearrange("h i j -> i h j"), W=[w_i.b])
    g_bc = P.tile([128, 1024], F32)
    P.dma("sp", g_bc.ap, bcast_rows(I["norm_mix"][0], D), W=[g_bc.b])
    cw = P.tile([128, 4, 8], F32)
    P.dma("sp", cw.ap, I["a_conv_w"][0].rearrange("k (c p) -> p k c", p=128), W=[cw.b], allow_slow_non_contiguous=True)
    pv = P.tile([128, 4, 8], F32)
    for i, nm in enumerate(["a_conv_b", "a_b_r", "a_b_i", "a_lambda"]):
        P.dma("sp", pv.ap[:, i, :], I[nm][0].rearrange("(c p) -> p c", p=128), W=[pv.b], allow_slow_non_contiguous=True)
    clam = P.tile([128, 8], F32)
    tmp8 = P.tile([128, 8], F32)
    P.act(tmp8.ap, pv.ap[:, 3, :], AF.Exp, R=[pv.b], W=[tmp8.b], scale=-1.0)
    P.act(tmp8.ap, tmp8.ap, AF.Ln, R=[tmp8.b, c["one"].b], W=[tmp8.b], bias=c["one"].ap)
    P.ts("dve", clam.ap, tmp8.ap, -8.0, None, ALU.mult, None, [tmp8.b], [clam.b])

    hts = P.tiles(2, [128, 4, 1024], F32)
    hns = P.tiles(2, [128, 1024], BF16)
    hnTs = P.tiles(2, [128, 8, 512], BF16)
    junk = P.tile([128, 1024], BF16)
    ss = P.tile([128, 4], F32)
    rstd = P.tile([128, 4], F32)
    gys = P.tiles(2, [128, 512], F32)
    xpads = P.tiles(2, [128, 515], F32)
    xcarry = P.tile([128, 8, 3], F32)
    hprev = P.tile([128, 8], F32)
    xcs = P.tiles(2, [128, 512], F32)
    xcbs = P.tiles(2, [128, 512], BF16)
    rs = P.tiles(2, [128, 512], F32)
    igs = P.tiles(2, [128, 512], F32)
    as_ = P.tiles(2, [128, 512], F32)
    ms = P.tiles(2, [128, 512], F32)
    us = P.tiles(2, [128, 512], F32)
    hss = P.tiles(2, [128, 512], F32)
    gTs = P.tiles(2, [128, 8, 512], BF16)
    psT = [P.psb[0], P.psb[1]]
    psP = [P.psb[2], P.psb[3]]
    psG = [P.psb[4], P.psb[5]]
    psO = [P.psb[6], P.psb[7]]
    it = 0
    pj = 0
    for s_ in range(BPC):
        P.memset("dve", xcarry.ap, 0.0, [xcarry.b])
        P.memset("dve", hprev.ap, 0.0, [hprev.b])
        for ch in range(4):
            r0 = s_ * S + ch * 512
            ti0 = r0 // 128
            ht = hts[it % 2]
            hnT = hnTs[it % 2]
            gT = gTs[it % 2]
            P.dma("sp", ht.ap, hin.ap[r0:r0 + 512, :].rearrange("(tt p) d -> p tt d", p=128),
                  R=hin.bufs[ti0:ti0 + 4], W=[ht.b])
            for tt in range(4):
                hn = hns[tt % 2]
                rms_rstd(P, c, ht.ap[:, tt, :], ht.b, junk, ss, rstd, tt)
                P.stt("dve", hn.ap, ht.ap[:, tt, :], rstd.ap[:, tt:tt + 1], g_bc.ap, ALU.mult, ALU.mult,
                      [ht.b, rstd.b, g_bc.b], [hn.b])
                transpose_tile(P, c, hn, hnT.ap[:, :, tt * 128:(tt + 1) * 128], hnT.b, 8, psT[tt % 2], "act")
            for j in range(8):
                gy = gys[pj % 2]
                xpad = xpads[pj % 2]
                xc = xcs[pj % 2]
                xcb = xcbs[pj % 2]
                r_ = rs[pj % 2]
                ig = igs[pj % 2]
                a_ = as_[pj % 2]
                m_ = ms[pj % 2]
                u_ = us[pj % 2]
                hs = hss[pj % 2]
                pg0 = psG[0]
                pg1 = psG[1]
                py = psP[0]
                for kc in range(8):
                    P.mm(py.ap, w_in.ap[:, kc, j * 128:(j + 1) * 128], hnT.ap[:, kc, :], kc == 0, kc == 7,
                         [w_in.b, hnT.b], [py.b])
                P.act(gy.ap, py.ap, AF.Gelu_apprx_tanh, R=[py.b], W=[gy.b])
                px = psP[1]
                for kc in range(8):
                    P.mm(px.ap, w_in.ap[:, kc, 1024 + j * 128:1024 + (j + 1) * 128], hnT.ap[:, kc, :], kc == 0, kc == 7,
                         [w_in.b, hnT.b], [px.b])
                P.cp("dve", xpad.ap[:, 0:3], xcarry.ap[:, j, :], [xcarry.b], [xpad.b])
                P.cp("act", xpad.ap[:, 3:515], px.ap, [px.b], [xpad.b])
                P.cp("dve", xcarry.ap[:, j, :], xpad.ap[:, 512:515], [xpad.b], [xcarry.b])
                P.ts("dve", xc.ap, xpad.ap[:, 0:512], cw.ap[:, 0, j:j + 1], pv.ap[:, 0, j:j + 1], ALU.mult, ALU.add,
                     [xpad.b, cw.b, pv.b], [xc.b])
                for k in range(1, 4):
                    P.stt("dve", xc.ap, xpad.ap[:, k:k + 512], cw.ap[:, k, j:j + 1], xc.ap, ALU.mult, ALU.add,
                          [xpad.b, cw.b, xc.b], [xc.b])
                P.cp("pool", xcb.ap, xc.ap, [xc.b], [xcb.b])
                P.mm(pg0.ap, w_r.ap[:, j, :], xcb.ap, True, True, [w_r.b, xcb.b], [pg0.b])
                P.mm(pg1.ap, w_i.ap[:, j, :], xcb.ap, True, True, [w_i.b, xcb.b], [pg1.b])
                P.act(r_.ap, pg0.ap, AF.Sigmoid, R=[pg0.b, pv.b], W=[r_.b], bias=pv.ap[:, 1, j:j + 1])
                P.act(ig.ap, pg1.ap, AF.Sigmoid, R=[pg1.b, pv.b], W=[ig.b], bias=pv.ap[:, 2, j:j + 1])
                P.act(a_.ap, r_.ap, AF.Exp, R=[r_.b, clam.b], W=[a_.b], scale=clam.ap[:, j:j + 1])
                P.tt("pool", m_.ap, a_.ap, a_.ap, ALU.mult, [a_.b], [m_.b])
                P.act(m_.ap, m_.ap, AF.Sqrt, R=[m_.b, c["one"].b], W=[m_.b], scale=-1.0, bias=c["one"].ap)
                P.tt("pool", u_.ap, ig.ap, xc.ap, ALU.mult, [ig.b, xc.b], [u_.b])
                P.tt("dve", u_.ap, u_.ap, m_.ap, ALU.mult, [u_.b, m_.b], [u_.b])
                P.op("dve", lambda e, hs=hs, a_=a_, u_=u_, j=j: e.tensor_tensor_scan(
                    out=hs.ap, data0=a_.ap, data1=u_.ap, initial=hprev.ap[:, j:j + 1], op0=ALU.mult, op1=ALU.add),
                    [a_.b, u_.b, hprev.b], [hs.b])
                P.cp("dve", hprev.ap[:, j:j + 1], hs.ap[:, 511:512], [hs.b], [hprev.b])
                P.tt("dve", gT.ap[:, j, :], gy.ap, hs.ap, ALU.mult, [gy.b, hs.b], [gT.b])
                pj += 1
            for tt in range(4):
                for nh in range(2):
                    po = psO[nh]
                    for j in range(8):
                        P.mm(po.ap, gT.ap[:, j, tt * 128:(tt + 1) * 128], w_out.ap[:, j, nh * 512:(nh + 1) * 512],
                             j == 0, j == 7, [gT.b, w_out.b], [po.b])
                    P.tt("dve", ht.ap[:, tt, nh * 512:(nh + 1) * 512], po.ap, ht.ap[:, tt, nh * 512:(nh + 1) * 512],
                         ALU.add, [po.b, ht.b], [ht.b])
            P.dma("sp", hout.ap[r0:r0 + 512, :].rearrange("(tt p) d -> p tt d", p=128), ht.ap,
                  R=[ht.b], W=hout.bufs[ti0:ti0 + 4])
            it += 1


NEG = -1.0e30


def stage_peer(P, I, L, hin, hout, XTd, Gd):
    for s_ in range(BPC):
        peer_route(P, I, L, hin, XTd, Gd, s_)
        peer_main(P, I, L, hin, hout, XTd, Gd, s_)


def peer_route(P, I, L, hin, XTd, Gd, s_):
    P.stage_begin()
    c = load_consts(P, I["consts"])
    identf = P.tile([128, 128], F32)
    P.dma("sp", identf.ap, I["consts"][:, 0:128], W=[identf.b])
    g_bc = P.tile([128, 1024], F32)
    P.dma("sp", g_bc.ap, bcast_rows(I["norm_ffn"][L], D), W=[g_bc.b])
    w_q = P.tile([128, 8, 2048], BF16)
    P.dma("pool", w_q.ap, I["peer_w_q"][L].rearrange("(kc p) n -> p kc n", p=128), W=[w_q.b])
    KT = P.tile([128, 16, 128], BF16)
    ps0, ps1 = P.psb[0], P.psb[1]
    psAB = [P.psb[2], P.psb[3], P.psb[4], P.psb[5]]
    psG = [P.psb[6], P.psb[7]]
    Gq = P.tiles(2, [128, 32, 128], BF16)
    Aq = P.tiles(1, [128, 128, 32], BF16)
    Apq = P.tiles(2, [128, 32, 128], BF16)
    Bg = P.tiles(2, [128, 128, 32], BF16)
    Bp = P.tile([128, 128, 128], BF16)
    Tt = P.tile([128, 8, 16, 32], F32)
    Ksb = Tl(Bp.ap.rearrange("p a b -> p (a b)")[:, 0:2048].rearrange("p (a b) -> p a b", a=16), Bp.b)
    P.dma("pool", Ksb.ap, I["peer_sub_keys"][L].rearrange("h p n d -> n (h p) d"), W=[Ksb.b])
    for half in range(2):
        pv = ps0.ap.bitcast(BF16)
        for q in range(8):
            hp = half * 8 + q
            P.tr(pv[:, q * 128:(q + 1) * 128], Ksb.ap[:, hp, :], c["ident"].ap, [Ksb.b, c["ident"].b], [ps0.b])
        P.cp("act", KT.ap[:, half * 8:(half + 1) * 8, :], pv.rearrange("p (a b) -> p a b", a=8), [ps0.b], [KT.b])
    ht = P.tile([128, 1024], F32)
    hn = P.tile([128, 1024], BF16)
    junk = Tl(Aq[0].ap.rearrange("p a b -> p (a b)")[:, 0:1024], Aq[0].b)
    ss = P.tile([128, 2], F32)
    rstd = P.tile([128, 2], F32)
    XTt = P.tile([128, 8, 128], BF16)
    qT = P.tile([128, 16, 128], BF16)
    scr = P.tiles(2, [128, 256], F32)
    cand = P.tile([128, 8, 16, 16], F32)
    jk16 = P.tile([128, 16], F32)

    class St:
        pass
    sts = []
    for _ in range(2):
        st = St()
        st.s_sb = P.tile([128, 16, 128], F32)
        st.sv = P.tile([128, 16, 16], F32)
        st.cv = P.tile([128, 8, 16], F32)
        st.negm = P.tile([128, 8], F32)
        st.negm1 = P.tile([128, 8], F32)
        st.negm2 = P.tile([128, 8], F32)
        st.Z = P.tile([128, 8], F32)
        st.rZ = P.tile([128, 8], F32)
        st.cc = P.tile([128, 8, 16], F32)
        st.e2 = P.tile([128, 8, 128], F32)
        st.cT = P.tile([128, 128], F32)
        st.sv4 = st.sv.ap.rearrange("p (h two) a -> p h two a", two=2)
        st.s4 = st.s_sb.ap.rearrange("p (h two) n -> p h two n", two=2)
        sts.append(st)
    cnt = {"ev": 0, "nb": 0}

    def front(tl_):
        st = sts[tl_ % 2]
        ti = s_ * 16 + tl_
        r0 = ti * 128
        s_sb, sv, cv = st.s_sb, st.sv, st.cv

        def fa():
            P.dma("sp", ht.ap, hin.ap[r0:r0 + 128, :], R=[hin.bufs[ti]], W=[ht.b])
            rms_rstd(P, c, ht.ap, ht.b, junk, ss, rstd, 0)
            P.stt("dve", hn.ap, ht.ap, rstd.ap[:, 0:1], g_bc.ap, ALU.mult, ALU.mult, [ht.b, rstd.b, g_bc.b], [hn.b])
            transpose_tile(P, c, hn, XTt.ap, XTt.b, 8, ps0, "act")
            P.dma("sp", XTd.ap[:, :, r0:r0 + 128], XTt.ap, R=[XTt.b], W=[XTd.bufs[ti]])
            for g in range(4):
                pq = ps1
                for u in range(4):
                    hp = g * 4 + u
                    for kc in range(8):
                        P.mm(pq.ap[:, u * 128:(u + 1) * 128], w_q.ap[:, kc, hp * 128:(hp + 1) * 128], XTt.ap[:, kc, :],
                             kc == 0, kc == 7, [w_q.b, XTt.b], [pq.b])
                P.cp("act", qT.ap[:, g * 4:(g + 1) * 4, :], pq.ap.rearrange("p (a b) -> p a b", a=4), [pq.b], [qT.b])
            for g in range(4):
                pq = ps0
                for u in range(4):
                    hp = g * 4 + u
                    P.mm(pq.ap[:, u * 128:(u + 1) * 128], qT.ap[:, hp, :], KT.ap[:, hp, :], True, True, [qT.b, KT.b], [pq.b])
                P.cp("act", s_sb.ap[:, g * 4:(g + 1) * 4, :], pq.ap.rearrange("p (a b) -> p a b", a=4), [pq.b], [s_sb.b])

        def topk(hps):
            for hp in hps:
                sc = scr[hp % 2]
                P.op("dve", lambda e, hp=hp: e.max(out=sv.ap[:, hp, 0:8], in_=s_sb.ap[:, hp, :]), [s_sb.b], [sv.b])
                P.op("dve", lambda e, hp=hp, sc=sc: e.match_replace(out=sc.ap[:, 0:128], in_to_replace=sv.ap[:, hp, 0:8],
                                                                   in_values=s_sb.ap[:, hp, :], imm_value=NEG),
                     [s_sb.b, sv.b], [sc.b])
                P.op("dve", lambda e, hp=hp, sc=sc: e.max(out=sv.ap[:, hp, 8:16], in_=sc.ap[:, 0:128]), [sc.b], [sv.b])

        def fd():
            P.tt("dve", cand.ap, st.sv4[:, :, 0, :].unsqueeze(3).to_broadcast([128, 8, 16, 16]),
                 st.sv4[:, :, 1, :].unsqueeze(2).to_broadcast([128, 8, 16, 16]), ALU.add, [sv.b], [cand.b])
            for h in range(8):
                sc = scr[h % 2]
                cflat = cand.ap[:, h, :, :].rearrange("p a b -> p (a b)")
                P.op("dve", lambda e, h=h, cflat=cflat: e.max(out=cv.ap[:, h, 0:8], in_=cflat), [cand.b], [cv.b])
                P.op("dve", lambda e, h=h, sc=sc, cflat=cflat: e.match_replace(out=sc.ap, in_to_replace=cv.ap[:, h, 0:8],
                                                                              in_values=cflat, imm_value=NEG),
                     [cand.b, cv.b], [sc.b])
                P.op("dve", lambda e, h=h, sc=sc: e.max(out=cv.ap[:, h, 8:16], in_=sc.ap), [sc.b], [cv.b])

        def fe():
            negm, negm1, negm2, Z, rZ, cc, e2, cT = st.negm, st.negm1, st.negm2, st.Z, st.rZ, st.cc, st.e2, st.cT
            P.ts("dve", negm1.ap, st.sv4[:, :, 0, 0], -1.0, None, ALU.mult, None, [sv.b], [negm1.b])
            P.ts("dve", negm2.ap, st.sv4[:, :, 1, 0], -1.0, None, ALU.mult, None, [sv.b], [negm2.b])
            P.tt("dve", negm.ap, negm1.ap, negm2.ap, ALU.add, [negm1.b, negm2.b], [negm.b])
            for h in range(8):
                P.act(jk16.ap, cv.ap[:, h, :], AF.Exp, R=[cv.b, negm.b], W=[jk16.b, Z.b], bias=negm.ap[:, h:h + 1],
                      accum_out=Z.ap[:, h:h + 1])
            P.op("dve", lambda e: e.reciprocal(out=rZ.ap, in_=Z.ap), [Z.b], [rZ.b])
            for h in range(8):
                P.act(cc.ap[:, h, :], sv.ap[:, 2 * h, :], AF.Exp, R=[sv.b, negm1.b], W=[cc.b], bias=negm1.ap[:, h:h + 1])
            for h in range(8):
                P.act(e2.ap[:, h, :], s_sb.ap[:, 2 * h + 1, :], AF.Exp, R=[s_sb.b, negm2.b], W=[e2.b], bias=negm2.ap[:, h:h + 1])
            P.tt("dve", cc.ap, cc.ap, rZ.ap.unsqueeze(2).to_broadcast([128, 8, 16]), ALU.mult, [cc.b, rZ.b], [cc.b])
            P.tr(ps1.ap[:, 0:128], cc.ap.rearrange("p h a -> p (h a)"), identf.ap, [cc.b, identf.b], [ps1.b])
            P.cp("act", cT.ap, ps1.ap[:, 0:128], [ps1.b], [cT.b])

        return [fa, lambda: topk(range(0, 8)), lambda: topk(range(8, 16)), fd, fe]

    def back(tl_):
        st = sts[tl_ % 2]
        ti = s_ * 16 + tl_
        s_sb, sv, cv, e2, cT, s4, sv4 = st.s_sb, st.sv, st.cv, st.e2, st.cT, st.s4, st.sv4

        def bgrp(g):
            bg = Bg[g % 2]
            i0 = g * 32
            P.tt("pool", Tt.ap, s4[:, :, 1, i0:i0 + 32].unsqueeze(2).to_broadcast([128, 8, 16, 32]),
                 sv4[:, :, 0, :].unsqueeze(3).to_broadcast([128, 8, 16, 32]), ALU.add, [s_sb.b, sv.b], [Tt.b])
            P.tt("dve", Tt.ap, Tt.ap, cv.ap[:, :, 15:16].unsqueeze(3).to_broadcast([128, 8, 16, 32]), ALU.is_ge,
                 [Tt.b, cv.b], [Tt.b])
            P.tt("pool", bg.ap.rearrange("p (h a) i -> p h a i", h=8), Tt.ap,
                 e2.ap[:, :, i0:i0 + 32].unsqueeze(2).to_broadcast([128, 8, 16, 32]), ALU.mult, [Tt.b, e2.b], [bg.b])
            for blk in range(4):
                pb = psAB[cnt["nb"] % 4]
                cnt["nb"] += 1
                pbv = pb.ap.bitcast(BF16)
                for ii in range(8):
                    P.tr(pbv[:, ii * 128:(ii + 1) * 128], bg.ap[:, :, blk * 8 + ii], c["ident"].ap,
                         [bg.b, c["ident"].b], [pb.b])
                P.cp("act", Bp.ap[:, i0 + blk * 8:i0 + blk * 8 + 8, :],
                     pbv.rearrange("p (a b) -> p a b", a=8), [pb.b], [Bp.b])

        def aqr(q4):
            aq = Aq[0]
            apq = Apq[q4 % 2]
            gq = Gq[q4 % 2]
            i0 = q4 * 32
            P.tt("dve", aq.ap.rearrange("p (h a) i -> p h a i", h=8),
                 s4[:, :, 0, i0:i0 + 32].unsqueeze(2).to_broadcast([128, 8, 16, 32]),
                 sv4[:, :, 0, :].unsqueeze(3).to_broadcast([128, 8, 16, 32]), ALU.is_equal, [s_sb.b, sv.b], [aq.b])
            for blk in range(4):
                pb = psAB[cnt["nb"] % 4]
                cnt["nb"] += 1
                pbv = pb.ap.bitcast(BF16)
                for ii in range(8):
                    P.tr(pbv[:, ii * 128:(ii + 1) * 128], aq.ap[:, :, blk * 8 + ii], c["ident"].ap,
                         [aq.b, c["ident"].b], [pb.b])
                P.tt("dve", apq.ap[:, blk * 8:blk * 8 + 8, :], pbv.rearrange("p (a b) -> p a b", a=8),
                     cT.ap.unsqueeze(1).to_broadcast([128, 8, 128]), ALU.mult, [pb.b, cT.b], [apq.b])
            for tg in range(8):
                pg = psG[tg % 2]
                for tt in range(16):
                    t = tg * 16 + tt
                    P.mm(pg.ap[:, tt * 32:(tt + 1) * 32], Bp.ap[:, :, t], apq.ap[:, :, t], True, True,
                         [Bp.b, apq.b], [pg.b])
                P.cp("act", gq.ap[:, :, tg * 16:(tg + 1) * 16],
                     pg.ap.rearrange("p (t i) -> p i t", t=16), [pg.b], [gq.b])
            for hh in range(2):
                a0 = i0 + hh * 16
                P.dma("sp", Gd.ap[ti, a0:a0 + 16].rearrange("a b t -> b a t"), gq.ap[:, hh * 16:(hh + 1) * 16, :],
                      R=[gq.b], W=[Gd.bufs[ti]])

        return [lambda: bgrp(0), lambda: bgrp(1), lambda: bgrp(2), lambda: bgrp(3),
                lambda: aqr(0), lambda: aqr(1), lambda: aqr(2), lambda: aqr(3)]

    for f in front(0):
        f()
    for tl_ in range(16):
        bk = back(tl_)
        fr = front(tl_ + 1) if tl_ + 1 < 16 else []
        order = [bk[0]] + fr[0:1] + [bk[1]] + fr[1:2] + [bk[2]] + fr[2:3] + [bk[3]] + fr[3:4] + [bk[4]] + fr[4:5] + bk[5:]
        for f in order:
            f()


def peer_main(P, I, L, hin, hout, XTd, Gd, s_):
    P.stage_begin()
    c = load_consts(P, I["consts"])
    XT = P.tile([128, 8, S], BF16)
    t0 = s_ * 16
    P.dma("sp", XT.ap, XTd.ap[:, :, s_ * S:(s_ + 1) * S], R=XTd.bufs[t0:t0 + 16], W=[XT.b])
    oacc = P.tile([128, 16, 1024], F32)
    oaccb = [Buf() for _ in range(16)]
    NB = 4
    Ust = P.tiles(2, [128, 1024], BF16)
    UT = [P.tiles(NB, [128, 8, 128], BF16) for _ in range(2)]
    Vb = [P.tiles(NB, [128, 1024], BF16) for _ in range(2)]
    Gb = [P.tiles(NB, [128, 16, 128], BF16) for _ in range(2)]
    actb = P.tiles(3, [128, 256], BF16)
    Wb = P.tiles(3, [128, 256], BF16)
    psS = [P.psb[0], P.psb[1]]
    psT = P.psb[2]
    psO = [P.psb[3], P.psb[4], P.psb[5], P.psb[6], P.psb[7]]
    ngrp = 128 // NB

    def load_group(grp):
        par = grp % 2
        for bi in range(NB):
            i1 = grp * NB + bi
            ust = Ust[bi % 2]
            P.dma("pool", ust.ap, I["peer_u"][L, i1 * 128:(i1 + 1) * 128, :], W=[ust.b])
            P.dma("pool", Vb[par][bi].ap, I["peer_v"][L, i1 * 128:(i1 + 1) * 128, :], W=[Vb[par][bi].b])
            P.dma("sp", Gb[par][bi].ap, Gd.ap[t0:t0 + 16, i1].rearrange("n b t -> b n t"),
                  R=Gd.bufs[t0:t0 + 16], W=[Gb[par][bi].b])
            pv = psT.ap.bitcast(BF16)
            for dc in range(8):
                P.tr(pv[:, dc * 128:(dc + 1) * 128], ust.ap[:, dc * 128:(dc + 1) * 128], c["ident"].ap,
                     [ust.b, c["ident"].b], [psT.b])
            P.cp("act", UT[par][bi].ap, pv.rearrange("p (a b) -> p a b", a=8), [psT.b], [UT[par][bi].b])

    load_group(0)
    wi = 0
    for grp in range(ngrp):
        par = grp % 2
        if grp + 1 < ngrp:
            load_group(grp + 1)
        items = [(pair, bi) for pair in range(8) for bi in range(NB)]
        pend = None

        def emit_S(pair, bi, k):
            ps = psS[k % 2]
            for dc in range(8):
                P.mm(ps.ap[:, 0:256], UT[par][bi].ap[:, dc, :], XT.ap[:, dc, pair * 256:(pair + 1) * 256], dc == 0, dc == 7,
                     [UT[par][bi].b, XT.b], [ps.b])
            a_ = actb[k % 3]
            w_ = Wb[k % 3]
            P.act(a_.ap, ps.ap[:, 0:256], AF.Gelu_apprx_tanh, R=[ps.b], W=[a_.b])
            P.tt("pool" if k % 2 == 0 else "dve", w_.ap, a_.ap,
                 Gb[par][bi].ap[:, 2 * pair:2 * pair + 2, :].rearrange("p a b -> p (a b)"), ALU.mult,
                 [a_.b, Gb[par][bi].b], [w_.b])
            return w_

        def emit_O(pair, bi, w_):
            for tt in range(2):
                for nh in range(2):
                    po = psO[(4 * (grp * 8 + pair) + tt * 2 + nh) % 5]
                    P.mm(po.ap, w_.ap[:, tt * 128:(tt + 1) * 128], Vb[par][bi].ap[:, nh * 512:(nh + 1) * 512],
                         bi == 0, bi == NB - 1, [w_.b, Vb[par][bi].b], [po.b])
            if bi == NB - 1:
                for tt in range(2):
                    tix = 2 * pair + tt
                    for nh in range(2):
                        po = psO[(4 * (grp * 8 + pair) + tt * 2 + nh) % 5]
                        dst = oacc.ap[:, tix, nh * 512:(nh + 1) * 512]
                        if grp == 0:
                            P.cp("dve", dst, po.ap, [po.b], [oaccb[tix]])
                        else:
                            P.tt("dve", dst, po.ap, dst, ALU.add, [po.b, oaccb[tix]], [oaccb[tix]])

        for (pair, bi) in items:
            w_ = emit_S(pair, bi, wi)
            wi += 1
            if pend is not None:
                emit_O(*pend)
            pend = (pair, bi, w_)
        emit_O(*pend)
    hts = P.tiles(2, [128, 1024], F32)
    for tl_ in range(16):
        ti = t0 + tl_
        ht = hts[tl_ % 2]
        P.dma("sp", ht.ap, hin.ap[ti * 128:(ti + 1) * 128, :], R=[hin.bufs[ti]], W=[ht.b])
        P.tt("dve", ht.ap, ht.ap, oacc.ap[:, tl_, :], ALU.add, [ht.b, oaccb[tl_]], [ht.b])
        P.dma("sp", hout.ap[ti * 128:(ti + 1) * 128, :], ht.ap, R=[ht.b], W=[hout.bufs[ti]])


def stage_ple(P, I, L, hin, hout, final):
    P.stage_begin()
    c = load_consts(P, I["consts"])
    w_g = P.tile([128, 8, 1024], BF16)
    P.dma("pool", w_g.ap, I["ple_w_gate"][L].rearrange("(kc p) n -> p kc n", p=128), W=[w_g.b])
    w_p = P.tile([128, 2, 1024], BF16)
    P.dma("pool", w_p.ap, I["ple_w_proj"][L].rearrange("(kc p) n -> p kc n", p=128), W=[w_p.b])
    g_bc = P.tile([128, 1024], F32)
    P.dma("sp", g_bc.ap, bcast_rows(I["norm_ple"][L], D), W=[g_bc.b])
    gf_bc = P.tile([128, 1024], F32)
    P.dma("sp", gf_bc.ap, bcast_rows(I["final_norm"], D), W=[gf_bc.b])
    pflat = I["p"][L].rearrange("b s d -> (b s) d")
    hts = P.tiles(3, [128, 1024], F32)
    pts = P.tiles(2, [128, 256], F32)
    hns = P.tiles(2, [128, 1024], BF16)
    pbs = P.tiles(2, [128, 256], BF16)
    hnTs = P.tiles(2, [128, 8, 128], BF16)
    pTs = P.tiles(2, [128, 2, 128], BF16)
    sgs = P.tiles(2, [128, 512], F32)
    junk = P.tile([128, 1024], BF16)
    ss = P.tile([128, 2], F32)
    rstd = P.tile([128, 2], F32)
    outs = P.tiles(2, [128, 1024], F32)
    for ti in range(NT):
        ht = hts[ti % 3]
        pt = pts[ti % 2]
        hn = hns[ti % 2]
        pb = pbs[ti % 2]
        hnT = hnTs[ti % 2]
        pT = pTs[ti % 2]
        r0 = ti * 128
        P.dma("sp", ht.ap, hin.ap[r0:r0 + 128, :], R=[hin.bufs[ti]], W=[ht.b])
        P.dma("sp", pt.ap, pflat[r0:r0 + 128, :], W=[pt.b])
        rms_rstd(P, c, ht.ap, ht.b, junk, ss, rstd, 0)
        P.stt("dve", hn.ap, ht.ap, rstd.ap[:, 0:1], g_bc.ap, ALU.mult, ALU.mult, [ht.b, rstd.b, g_bc.b], [hn.b])
        transpose_tile(P, c, hn, hnT.ap, hnT.b, 8, P.psb[0], "act")
        P.cp("pool", pb.ap, pt.ap, [pt.b], [pb.b])
        transpose_tile(P, c, pb, pT.ap, pT.b, 2, P.psb[1], "act")
        for nh in range(2):
            pg = P.psb[2 + nh]
            pp = P.psb[4 + nh]
            sg = sgs[nh]
            for kc in range(8):
                P.mm(pg.ap, hnT.ap[:, kc, :], w_g.ap[:, kc, nh * 512:(nh + 1) * 512], kc == 0, kc == 7, [hnT.b, w_g.b], [pg.b])
            for kc in range(2):
                P.mm(pp.ap, pT.ap[:, kc, :], w_p.ap[:, kc, nh * 512:(nh + 1) * 512], kc == 0, kc == 1, [pT.b, w_p.b], [pp.b])
            P.act(sg.ap, pg.ap, AF.Sigmoid, R=[pg.b], W=[sg.b])
            P.tt("dve", sg.ap, sg.ap, pp.ap, ALU.mult, [sg.b, pp.b], [sg.b])
            P.tt("dve", ht.ap[:, nh * 512:(nh + 1) * 512], ht.ap[:, nh * 512:(nh + 1) * 512], sg.ap, ALU.add, [ht.b, sg.b], [ht.b])
        if final:
            ot = outs[ti % 2]
            rms_rstd(P, c, ht.ap, ht.b, junk, ss, rstd, 1)
            P.stt("dve", ot.ap, ht.ap, rstd.ap[:, 1:2], gf_bc.ap, ALU.mult, ALU.mult, [ht.b, rstd.b, gf_bc.b], [ot.b])
            P.dma("sp", hout.ap[r0:r0 + 128, :], ot.ap, R=[ot.b], W=[hout.bufs[ti]])
        else:
            P.dma("sp", hout.ap[r0:r0 + 128, :], ht.ap, R=[ht.b], W=[hout.bufs[ti]])


def stage_attn(P, I, hin, hout):
    for s_ in range(BPC):
        attn_seq(P, I, hin, hout, s_)


def attn_seq(P, I, hin, hout, s_):
    P.stage_begin()
    c = load_consts(P, I["consts"])
    negtri = P.tile([128, 128], BF16)
    P.dma("pool", negtri.ap, I["consts"][:, 128:256], W=[negtri.b])
    negones = P.tile([128, 128], BF16)
    P.dma("pool", negones.ap, I["consts"][:, 256:384], W=[negones.b])
    wm = P.tile([128, 896], BF16)
    P.dma("pool", wm.ap, I["consts"][:, 384:384 + 896], W=[wm.b])
    qT = P.tile([128, 8, S], BF16)
    kT = P.tile([128, 8, S], BF16)
    v = P.tile([128, 16, 1024], BF16)
    qb = [[[Buf(), Buf()] for _ in range(4)] for _ in range(8)]
    mark = P.aoff
    w_q = P.tile([128, 8, 1024], BF16)
    P.dma("pool", w_q.ap, I["b_w_q"][0].rearrange("(kc p) n -> p kc n", p=128), W=[w_q.b])
    w_kv = P.tile([128, 8, 2048], BF16)
    P.dma("pool", w_kv.ap, I["w_kv"].rearrange("(kc p) n -> p kc n", p=128), W=[w_kv.b])
    g1 = P.tile([128, 1024], F32)
    P.dma("sp", g1.ap, bcast_rows(I["norm_mix"][1], D), W=[g1.b])
    g2 = P.tile([128, 1024], F32)
    P.dma("sp", g2.ap, bcast_rows(I["kv_norm"], D), W=[g2.b])
    ht = P.tile([128, 4, 1024], F32)
    hn = P.tile([128, 1024], BF16)
    hkv = P.tile([128, 1024], BF16)
    hnT = P.tile([128, 8, 512], BF16)
    hkvT = P.tile([128, 8, 512], BF16)
    junk = P.tile([128, 1024], BF16)
    ss = P.tile([128, 4], F32)
    rstd = P.tile([128, 4], F32)
    for ch in range(4):
        r0 = s_ * S + ch * 512
        ti0 = r0 // 128
        P.dma("sp", ht.ap, hin.ap[r0:r0 + 512, :].rearrange("(tt p) d -> p tt d", p=128), R=hin.bufs[ti0:ti0 + 4], W=[ht.b])
        for tt in range(4):
            rms_rstd(P, c, ht.ap[:, tt, :], ht.b, junk, ss, rstd, tt)
            P.stt("dve", hn.ap, ht.ap[:, tt, :], rstd.ap[:, tt:tt + 1], g1.ap, ALU.mult, ALU.mult, [ht.b, rstd.b, g1.b], [hn.b])
            P.stt("dve", hkv.ap, ht.ap[:, tt, :], rstd.ap[:, tt:tt + 1], g2.ap, ALU.mult, ALU.mult, [ht.b, rstd.b, g2.b], [hkv.b])
            transpose_tile(P, c, hn, hnT.ap[:, :, tt * 128:(tt + 1) * 128], hnT.b, 8, P.psb[4], "act")
            transpose_tile(P, c, hkv, hkvT.ap[:, :, tt * 128:(tt + 1) * 128], hkvT.b, 8, P.psb[5], "act")
        for jc in range(8):
            pq = P.psb[6]
            for kc in range(8):
                P.mm(pq.ap, w_q.ap[:, kc, jc * 128:(jc + 1) * 128], hnT.ap[:, kc, :], kc == 0, kc == 7, [w_q.b, hnT.b], [pq.b])
            P.act(qT.ap[:, jc, ch * 512:(ch + 1) * 512], pq.ap, AF.Copy, R=[pq.b], W=qb[jc][ch], scale=0.125)
            pk = P.psb[7]
            for kc in range(8):
                P.mm(pk.ap, w_kv.ap[:, kc, jc * 128:(jc + 1) * 128], hkvT.ap[:, kc, :], kc == 0, kc == 7, [w_kv.b, hkvT.b], [pk.b])
            P.cp("dve", kT.ap[:, jc, ch * 512:(ch + 1) * 512], pk.ap, [pk.b], [kT.b])
        for tt in range(4):
            for nh in range(2):
                pv_ = P.psb[6 + nh]
                for kc in range(8):
                    P.mm(pv_.ap, hkvT.ap[:, kc, tt * 128:(tt + 1) * 128], w_kv.ap[:, kc, 1024 + nh * 512:1024 + (nh + 1) * 512],
                         kc == 0, kc == 7, [hkvT.b, w_kv.b], [pv_.b])
                P.cp("act" if nh == 0 else "dve", v.ap[:, ch * 4 + tt, nh * 512:(nh + 1) * 512], pv_.ap, [pv_.b], [v.b])
    P.barrier()
    P.aoff = mark
    w_o = P.tile([128, 8, 1024], BF16)
    P.dma("pool", w_o.ap, I["b_w_o"][0].rearrange("(kc p) n -> p kc n", p=128), W=[w_o.b])
    es = P.tiles(2, [128, 512], F32)
    sps = P.tiles(3, [128, 512], BF16)
    As = P.tiles(3, [128, 512], BF16)
    Srun = P.tile([128, 512], BF16)
    psZs = [P.psb[0], P.psb[1]]
    psOs = [P.psb[2], P.psb[3]]
    pairs = []
    no = 0
    for h in range(16):
        for cq in range(4):
            top = 4 * cq + 3
            pso = psOs[no % 2]
            no += 1
            for kb in range(top, -1, -1):
                pairs.append((h, cq, kb, top, pso))

    def stage_x(n):
        h, cq, kb, top, pso = pairs[n]
        jc, par = h // 2, h % 2
        po = par * 64
        psZ, e_, sp = psZs[n % 2], es[n % 2], sps[n % 3]
        P.mm(psZ.ap, kT.ap[po:po + 64, jc, kb * 128:(kb + 1) * 128], qT.ap[po:po + 64, jc, cq * 512:(cq + 1) * 512],
             True, False, [kT.b, qb[jc][cq][par]], [psZ.b])
        P.act(e_.ap, psZ.ap, AF.Exp, R=[psZ.b], W=[e_.b])
        P.act(sp.ap, e_.ap, AF.Ln, R=[e_.b, c["one"].b], W=[sp.b], bias=c["one"].ap)
        if kb >= 4 * cq:
            dl = cq * 512 - kb * 128
            P.tt("dve", sp.ap, sp.ap, wm.ap[:, 384 + dl:384 + dl + 512], ALU.mult, [sp.b, wm.b], [sp.b])

    def stage_y(n):
        h, cq, kb, top, pso = pairs[n]
        jc, par = h // 2, h % 2
        po = par * 64
        psZ, sp, A_ = psZs[n % 2], sps[n % 3], As[n % 3]
        P.mm(psZ.ap, negtri.ap, sp.ap, False, kb == top, [negtri.b, sp.b], [psZ.b])
        if kb != top:
            P.mm(psZ.ap, negones.ap, Srun.ap, False, True, [negones.b, Srun.b], [psZ.b])
        if kb == top:
            P.cp("pool", Srun.ap, sp.ap, [sp.b], [Srun.b])
        elif kb > 0:
            P.tt("pool", Srun.ap, Srun.ap, sp.ap, ALU.add, [Srun.b, sp.b], [Srun.b])
        P.act(A_.ap, psZ.ap, AF.Exp, R=[psZ.b], W=[A_.b])
        if kb >= 4 * cq:
            dl = cq * 512 - kb * 128
            P.tt("dve", A_.ap, A_.ap, wm.ap[:, 384 + dl:384 + dl + 512], ALU.mult, [A_.b, wm.b], [A_.b])
        P.mm(pso.ap[po:po + 64, :], v.ap[:, kb, h * 64:(h + 1) * 64], A_.ap, kb == top, kb == 0, [v.b, A_.b], [pso.b])
        if kb == 0:
            P.cp("dve", qT.ap[po:po + 64, jc, cq * 512:(cq + 1) * 512], pso.ap[po:po + 64, :], [pso.b], [qb[jc][cq][par]])

    stage_x(0)
    for n in range(len(pairs)):
        if n + 1 < len(pairs):
            stage_x(n + 1)
        stage_y(n)
    hts = P.tiles(2, [128, 1024], F32)
    for tl_ in range(16):
        ti = s_ * 16 + tl_
        cq = tl_ // 4
        ht2 = hts[tl_ % 2]
        P.dma("sp", ht2.ap, hin.ap[ti * 128:(ti + 1) * 128, :], R=[hin.bufs[ti]], W=[ht2.b])
        for nh in range(2):
            po_ = P.psb[4 + nh]
            for jc in range(8):
                P.mm(po_.ap, qT.ap[:, jc, tl_ * 128:(tl_ + 1) * 128], w_o.ap[:, jc, nh * 512:(nh + 1) * 512], jc == 0, jc == 7,
                     [qb[jc][cq][0], qb[jc][cq][1], w_o.b], [po_.b])
            P.tt("dve", ht2.ap[:, nh * 512:(nh + 1) * 512], ht2.ap[:, nh * 512:(nh + 1) * 512], po_.ap, ALU.add, [ht2.b, po_.b], [ht2.b])
        P.dma("sp", hout.ap[ti * 128:(ti + 1) * 128, :], ht2.ap, R=[ht2.b], W=[hout.bufs[ti]])


WEIGHT_SPECS = [
    ("norm_mix", [2, 1024]), ("a_w_in", [1, 1024, 2048]), ("a_conv_w", [1, 4, 1024]), ("a_conv_b", [1, 1024]),
    ("a_w_r", [1, 8, 128, 128]), ("a_w_i", [1, 8, 128, 128]), ("a_b_r", [1, 1024]), ("a_b_i", [1, 1024]),
    ("a_lambda", [1, 1024]), ("a_w_out", [1, 1024, 1024]), ("kv_norm", [1024]), ("w_kv", [1024, 2048]),
    ("b_w_q", [1, 1024, 1024]), ("b_w_o", [1, 1024, 1024]), ("norm_ffn", [2, 1024]), ("peer_w_q", [2, 1024, 2048]),
    ("peer_sub_keys", [2, 8, 2, 128, 128]), ("peer_u", [2, 16384, 1024]), ("peer_v", [2, 16384, 1024]),
    ("norm_ple", [2, 1024]), ("ple_w_gate", [2, 1024, 1024]), ("ple_w_proj", [2, 256, 1024]), ("final_norm", [1024]),
]
NCONST = 128 * 3 + 896


def make_consts():
    cst = np.zeros((128, NCONST), np.float32)
    cst[:, 0:128] = np.eye(128, dtype=np.float32)
    j = np.arange(128)[:, None]
    s_ = np.arange(128)[None, :]
    cst[:, 128:256] = -(j >= s_).astype(np.float32)
    cst[:, 256:384] = -1.0
    xx = np.arange(896)[None, :]
    cst[:, 384:384 + 896] = (j < xx - 384).astype(np.float32)
    return cst


def build(stages):
    nc = bass.Bass("TRN2", target_bir_lowering=False)
    I = {}
    I["x"] = nc.dram_tensor("x", [BPC, S, D], F32, kind="ExternalInput").ap()
    I["p"] = nc.dram_tensor("p", [2, BPC, S, PLE], F32, kind="ExternalInput").ap()
    for nm, shp in WEIGHT_SPECS:
        I[nm] = nc.dram_tensor(nm, shp, F32, kind="ExternalInput").ap()
    I["consts"] = nc.dram_tensor("consts", [128, NCONST], F32, kind="ExternalInput").ap()
    out = nc.dram_tensor("out", [T, D], F32, kind="ExternalOutput").ap()
    P = Prog(nc)
    hx = Dram(I["x"].rearrange("b s d -> (b s) d"), NT)
    hs = [Dram(nc.dram_tensor("hscr%d" % i, [T, D], F32, kind="Internal").ap(), NT) for i in range(2)]
    XTd = Dram(nc.dram_tensor("xtd", [128, 8, T], BF16, kind="Internal").ap(), NT)
    Gd = Dram(nc.dram_tensor("gd", [NT, 128, 128, 128], BF16, kind="Internal").ap(), NT)
    hout = Dram(out, NT)
    cur = hx
    nst = len(stages)
    for si, st in enumerate(stages):
        dst = hout if si == nst - 1 else hs[si % 2]
        if st == "rglru":
            stage_rglru(P, I, cur, dst)
        elif st in ("peer0", "peer1"):
            stage_peer(P, I, int(st[-1]), cur, dst, XTd, Gd)
        elif st in ("ple0", "ple1"):
            stage_ple(P, I, int(st[-1]), cur, dst, st == "ple1")
        elif st == "attn":
            stage_attn(P, I, cur, dst)
        else:
            raise ValueError(st)
        cur = dst
    P.barrier()
    return nc, P


ALL_STAGES = ["rglru", "peer0", "ple0", "attn", "peer1", "ple1"]


def kernel(**inputs):
    nc, P = build(ALL_STAGES)
    cst = make_consts()
    in_maps = []
    for c in range(NCORES):
        m = {"consts": cst}
        m["x"] = np.ascontiguousarray(inputs["x"][c * BPC:(c + 1) * BPC])
        m["p"] = np.ascontiguousarray(inputs["p"][:, c * BPC:(c + 1) * BPC])
        for nm, _ in WEIGHT_SPECS:
            m[nm] = np.ascontiguousarray(inputs[nm])
        in_maps.append(m)
    res = run_bass_kernel_spmd(nc, in_maps, core_ids=list(range(NCORES)))
    outs = [np.asarray(r["out"]).reshape(BPC, S, D) for r in res.results]
    return np.concatenate(outs, axis=0).astype(np.float32)
```

```python
import numpy as np
import concourse.bass as bass
import concourse.mybir as mybir
from concourse.bass import AP
from concourse.bass_utils import run_bass_kernel_spmd

F32 = mybir.dt.float32
BF16 = mybir.dt.bfloat16
U32 = mybir.dt.uint32
AF = mybir.ActivationFunctionType
ALU = mybir.AluOpType
AX = mybir.AxisListType

NCORES = 8
BPC = 2
S = 2048
D = 1024
T = BPC * S
NT = T // 128
PLE = 256
NKEY = 128
ARENA_WORDS = 52000
KDMA = 12
EPS = 1e-6
DSIZE = {F32: 4, BF16: 2, U32: 4}


class Buf:
    __slots__ = ("w", "r")

    def __init__(self):
        self.w = {}
        self.r = {}


class Tl:
    def __init__(self, ap, b=None):
        self.ap = ap
        self.b = b if b is not None else Buf()


class Dram:
    def __init__(self, ap, ntiles):
        self.ap = ap
        self.bufs = [Buf() for _ in range(ntiles)]


class Prog:
    CE = ("pe", "act", "dve", "pool")
    DQ = ("sp", "pool", "act")

    def __init__(self, nc):
        self.nc = nc
        self.eng = {"pe": nc.tensor, "act": nc.scalar, "dve": nc.vector, "pool": nc.gpsimd, "sp": nc.sync}
        self.semh = {}
        for e in self.CE:
            self.semh[e] = nc.alloc_semaphore("c_" + e)
        for q in self.DQ:
            for k in range(KDMA):
                self.semh["%s_d%d" % (q, k)] = nc.alloc_semaphore("d_%s_%d" % (q, k))
        self.cnt = {e: 0 for e in self.CE}
        self.dcnt = {q: 0 for q in self.DQ}
        self.seen = {e: {} for e in self.eng}
        self.arena = nc.alloc_sbuf_tensor("arena", [128, ARENA_WORDS], F32).ap()
        self.aoff = 0
        self.psb = [Tl(nc.alloc_psum_tensor("psb%d" % i, [128, 512], F32).ap()) for i in range(8)]
        self.ninst = 0

    def tile(self, shape, dtype=F32):
        n = 1
        for s_ in shape[1:]:
            n *= s_
        nbytes = (n * DSIZE[dtype] + 31) // 32 * 32
        words = nbytes // 4
        off = self.aoff
        self.aoff += words
        assert self.aoff <= ARENA_WORDS, ("arena overflow", self.aoff)
        v = self.arena[0:shape[0], off:off + n * DSIZE[dtype] // 4]
        if dtype != F32:
            v = v.bitcast(dtype)
        if len(shape) == 3:
            v = v.rearrange("p (a b) -> p a b", a=shape[1], b=shape[2])
        elif len(shape) == 4:
            v = v.rearrange("p (a b c) -> p a b c", a=shape[1], b=shape[2], c=shape[3])
        return Tl(v)

    def tiles(self, n, shape, dtype=F32):
        return [self.tile(shape, dtype) for _ in range(n)]

    def _wait(self, eng, sem, val):
        if val <= 0 or self.seen[eng].get(sem, 0) >= val:
            return
        self.seen[eng][sem] = val
        self.eng[eng].wait_ge(self.semh[sem], val)

    def _deps(self, eng, R, W):
        need = {}
        for b in R:
            for s_, v in b.w.items():
                if need.get(s_, 0) < v:
                    need[s_] = v
        for b in W:
            for s_, v in b.w.items():
                if need.get(s_, 0) < v:
                    need[s_] = v
            for s_, v in b.r.items():
                if need.get(s_, 0) < v:
                    need[s_] = v
        for s_, v in need.items():
            if eng == "pe" and s_ == "pe":
                continue
            self._wait(eng, s_, v)

    def _mark(self, sem, val, R, W):
        for b in W:
            b.w = {sem: val}
            b.r = {}
        for b in R:
            if b in W:
                continue
            if b.r.get(sem, 0) < val:
                b.r[sem] = val

    def op(self, eng, fn, R=(), W=()):
        self._deps(eng, R, W)
        self.cnt[eng] += 1
        fn(self.eng[eng]).then_inc(self.semh[eng], 1)
        self._mark(eng, self.cnt[eng], R, W)
        self.ninst += 1

    def dma(self, q, out, in_, R=(), W=(), **kw):
        self._deps(q, R, W)
        i = self.dcnt[q]
        self.dcnt[q] += 1
        k = i % KDMA
        val = 16 * (i // KDMA + 1)
        sem = "%s_d%d" % (q, k)
        self._wait(q, sem, val - 16)
        self.eng[q].dma_start(out=out, in_=in_, **kw).then_inc(self.semh[sem], 16)
        self._mark(sem, val, R, W)
        self.ninst += 1

    def barrier(self):
        for e in self.eng:
            for s_ in self.CE:
                self._wait(e, s_, self.cnt[s_])
            for q in self.DQ:
                for k in range(KDMA):
                    i = self.dcnt[q]
                    n = i // KDMA + (1 if k < i % KDMA else 0)
                    self._wait(e, "%s_d%d" % (q, k), 16 * n)

    def stage_begin(self):
        self.barrier()
        self.aoff = 0

    def mm(self, out, lhsT, rhs, start, stop, R, W):
        self.op("pe", lambda e: e.matmul(out, lhsT=lhsT, rhs=rhs, start=start, stop=stop), R, W)

    def tr(self, out, in_, ident, R, W):
        self.op("pe", lambda e: e.transpose(out, in_, ident), R, W)

    def act(self, out, in_, func, R, W, **kw):
        self.op("act", lambda e: e.activation(out=out, in_=in_, func=func, **kw), R, W)

    def tt(self, eng, out, in0, in1, op, R, W):
        self.op(eng, lambda e: e.tensor_tensor(out=out, in0=in0, in1=in1, op=op), R, W)

    def ts(self, eng, out, in0, s1, s2, op0, op1, R, W):
        if op1 is None:
            self.op(eng, lambda e: e.tensor_scalar(out=out, in0=in0, scalar1=s1, scalar2=None, op0=op0), R, W)
        else:
            self.op(eng, lambda e: e.tensor_scalar(out=out, in0=in0, scalar1=s1, scalar2=s2, op0=op0, op1=op1), R, W)

    def stt(self, eng, out, in0, scalar, in1, op0, op1, R, W):
        self.op(eng, lambda e: e.scalar_tensor_tensor(out=out, in0=in0, scalar=scalar, in1=in1, op0=op0, op1=op1), R, W)

    def cp(self, eng, out, in_, R, W):
        if eng == "act":
            self.op("act", lambda e: e.activation(out=out, in_=in_, func=AF.Copy), R, W)
        else:
            self.op(eng, lambda e: e.tensor_copy(out=out, in_=in_), R, W)

    def memset(self, eng, ap, val, W):
        self.op(eng, lambda e: e.memset(ap, val), (), W)


def bcast_rows(dram_ap_1d, n):
    return AP(dram_ap_1d.tensor, dram_ap_1d.offset, [[0, 128], [1, n]])


class Common:
    pass


def load_consts(P, consts):
    c = {}
    ident = P.tile([128, 128], BF16)
    P.dma("pool", ident.ap, consts[:, 0:128], W=[ident.b])
    c["ident"] = ident
    one = P.tile([128, 1], F32)
    P.memset("dve", one.ap, 1.0, [one.b])
    c["one"] = one
    eps = P.tile([128, 1], F32)
    P.memset("dve", eps.ap, EPS, [eps.b])
    c["eps"] = eps
    return c


def rms_rstd(P, c, x_ap, xb, junk, ss, rstd, col):
    P.act(junk.ap, x_ap, AF.Square, R=[xb], W=[junk.b, ss.b], accum_out=ss.ap[:, col:col + 1])
    P.act(rstd.ap[:, col:col + 1], ss.ap[:, col:col + 1], AF.Sqrt, R=[ss.b, c["eps"].b], W=[rstd.b],
          scale=1.0 / D, bias=c["eps"].ap)
    P.op("dve", lambda e: e.reciprocal(out=rstd.ap[:, col:col + 1], in_=rstd.ap[:, col:col + 1]), [rstd.b], [rstd.b])


def transpose_tile(P, c, src, dst_ap, dstb, nk, psT, evac_eng):
    pv = psT.ap.bitcast(BF16)
    for kc in range(nk):
        P.tr(pv[:, kc * 128:(kc + 1) * 128], src.ap[:, kc * 128:(kc + 1) * 128], c["ident"].ap,
             R=[src.b, c["ident"].b], W=[psT.b])
    P.cp(evac_eng, dst_ap, pv[:, 0:nk * 128].rearrange("p (k t) -> p k t", k=nk), [psT.b], [dstb])


def stage_rglru(P, I, hin, hout):
    nc = P.nc
    P.stage_begin()
    c = load_consts(P, I["consts"])
    w_in = P.tile([128, 8, 2048], BF16)
    P.dma("pool", w_in.ap, I["a_w_in"][0].rearrange("(kc p) n -> p kc n", p=128), W=[w_in.b])
    w_out = P.tile([128, 8, 1024], BF16)
    P.dma("pool", w_out.ap, I["a_w_out"][0].rearrange("(kc p) n -> p kc n", p=128), W=[w_out.b])
    w_r = P.tile([128, 8, 128], BF16)
    P.dma("pool", w_r.ap, I["a_w_r"][0].rearrange("h i j -> i h j"), W=[w_r.b])
    w_i = P.tile([128, 8, 128], BF16)
    P.dma("pool", w_i.ap, I["a_w_i"][0].rearrange("h i j -> i h j"), W=[w_i.b])
    g_bc = P.tile([128, 1024], F32)
    P.dma("sp", g_bc.ap, bcast_rows(I["norm_mix"][0], D), W=[g_bc.b])
    cw = P.tile([128, 4, 8], F32)
    P.dma("sp", cw.ap, I["a_conv_w"][0].rearrange("k (c p) -> p k c", p=128), W=[cw.b], allow_slow_non_contiguous=True)
    pv = P.tile([128, 4, 8], F32)
    for i, nm in enumerate(["a_conv_b", "a_b_r", "a_b_i", "a_lambda"]):
        P.dma("sp", pv.ap[:, i, :], I[nm][0].rearrange("(c p) -> p c", p=128), W=[pv.b], allow_slow_non_contiguous=True)
    clam = P.tile([128, 8], F32)
    tmp8 = P.tile([128, 8], F32)
    P.act(tmp8.ap, pv.ap[:, 3, :], AF.Exp, R=[pv.b], W=[tmp8.b], scale=-1.0)
    P.act(tmp8.ap, tmp8.ap, AF.Ln, R=[tmp8.b, c["one"].b], W=[tmp8.b], bias=c["one"].ap)
    P.ts("dve", clam.ap, tmp8.ap, -8.0, None, ALU.mult, None, [tmp8.b], [clam.b])

    hts = P.tiles(2, [128, 4, 1024], F32)
    hns = P.tiles(2, [128, 1024], BF16)
    hnTs = P.tiles(2, [128, 8, 512], BF16)
    junk = P.tile([128, 1024], BF16)
    ss = P.tile([128, 4], F32)
    rstd = P.tile([128, 4], F32)
    gys = P.tiles(2, [128, 512], F32)
    xpads = P.tiles(2, [128, 515], F32)
    xcarry = P.tile([128, 8, 3], F32)
    hprev = P.tile([128, 8], F32)
    xcs = P.tiles(2, [128, 512], F32)
    xcbs = P.tiles(2, [128, 512], BF16)
    rs = P.tiles(2, [128, 512], F32)
    igs = P.tiles(2, [128, 512], F32)
    as_ = P.tiles(2, [128, 512], F32)
    ms = P.tiles(2, [128, 512], F32)
    us = P.tiles(2, [128, 512], F32)
    hss = P.tiles(2, [128, 512], F32)
    gTs = P.tiles(2, [128, 8, 512], BF16)
    psT = [P.psb[0], P.psb[1]]
    psP = [P.psb[2], P.psb[3]]
    psG = [P.psb[4], P.psb[5]]
    psO = [P.psb[6], P.psb[7]]
    it = 0
    pj = 0
    for s_ in range(BPC):
        P.memset("dve", xcarry.ap, 0.0, [xcarry.b])
        P.memset("dve", hprev.ap, 0.0, [hprev.b])
        for ch in range(4):
            r0 = s_ * S + ch * 512
            ti0 = r0 // 128
            ht = hts[it % 2]
            hnT = hnTs[it % 2]
            gT = gTs[it % 2]
            P.dma("sp", ht.ap, hin.ap[r0:r0 + 512, :].rearrange("(tt p) d -> p tt d", p=128),
                  R=hin.bufs[ti0:ti0 + 4], W=[ht.b])
            for tt in range(4):
                hn = hns[tt % 2]
                rms_rstd(P, c, ht.ap[:, tt, :], ht.b, junk, ss, rstd, tt)
                P.stt("dve", hn.ap, ht.ap[:, tt, :], rstd.ap[:, tt:tt + 1], g_bc.ap, ALU.mult, ALU.mult,
                      [ht.b, rstd.b, g_bc.b], [hn.b])
                transpose_tile(P, c, hn, hnT.ap[:, :, tt * 128:(tt + 1) * 128], hnT.b, 8, psT[tt % 2], "act")
            for j in range(8):
                gy = gys[pj % 2]
                xpad = xpads[pj % 2]
                xc = xcs[pj % 2]
                xcb = xcbs[pj % 2]
                r_ = rs[pj % 2]
                ig = igs[pj % 2]
                a_ = as_[pj % 2]
                m_ = ms[pj % 2]
                u_ = us[pj % 2]
                hs = hss[pj % 2]
                pg0 = psG[0]
                pg1 = psG[1]
                py = psP[0]
                for kc in range(8):
                    P.mm(py.ap, w_in.ap[:, kc, j * 128:(j + 1) * 128], hnT.ap[:, kc, :], kc == 0, kc == 7,
                         [w_in.b, hnT.b], [py.b])
                P.act(gy.ap, py.ap, AF.Gelu_apprx_tanh, R=[py.b], W=[gy.b])
                px = psP[1]
                for kc in range(8):
                    P.mm(px.ap, w_in.ap[:, kc, 1024 + j * 128:1024 + (j + 1) * 128], hnT.ap[:, kc, :], kc == 0, kc == 7,
                         [w_in.b, hnT.b], [px.b])
                P.cp("dve", xpad.ap[:, 0:3], xcarry.ap[:, j, :], [xcarry.b], [xpad.b])
                P.cp("act", xpad.ap[:, 3:515], px.ap, [px.b], [xpad.b])
                P.cp("dve", xcarry.ap[:, j, :], xpad.ap[:, 512:515], [xpad.b], [xcarry.b])
                P.ts("dve", xc.ap, xpad.ap[:, 0:512], cw.ap[:, 0, j:j + 1], pv.ap[:, 0, j:j + 1], ALU.mult, ALU.add,
                     [xpad.b, cw.b, pv.b], [xc.b])
                for k in range(1, 4):
                    P.stt("dve", xc.ap, xpad.ap[:, k:k + 512], cw.ap[:, k, j:j + 1], xc.ap, ALU.mult, ALU.add,
                          [xpad.b, cw.b, xc.b], [xc.b])
                P.cp("pool", xcb.ap, xc.ap, [xc.b], [xcb.b])
                P.mm(pg0.ap, w_r.ap[:, j, :], xcb.ap, True, True, [w_r.b, xcb.b], [pg0.b])
                P.mm(pg1.ap, w_i.ap[:, j, :], xcb.ap, True, True, [w_i.b, xcb.b], [pg1.b])
                P.act(r_.ap, pg0.ap, AF.Sigmoid, R=[pg0.b, pv.b], W=[r_.b], bias=pv.ap[:, 1, j:j + 1])
                P.act(ig.ap, pg1.ap, AF.Sigmoid, R=[pg1.b, pv.b], W=[ig.b], bias=pv.ap[:, 2, j:j + 1])
                P.act(a_.ap, r_.ap, AF.Exp, R=[r_.b, clam.b], W=[a_.b], scale=clam.ap[:, j:j + 1])
                P.tt("pool", m_.ap, a_.ap, a_.ap, ALU.mult, [a_.b], [m_.b])
                P.act(m_.ap, m_.ap, AF.Sqrt, R=[m_.b, c["one"].b], W=[m_.b], scale=-1.0, bias=c["one"].ap)
                P.tt("pool", u_.ap, ig.ap, xc.ap, ALU.mult, [ig.b, xc.b], [u_.b])
                P.tt("dve", u_.ap, u_.ap, m_.ap, ALU.mult, [u_.b, m_.b], [u_.b])
                P.op("dve", lambda e, hs=hs, a_=a_, u_=u_, j=j: e.tensor_tensor_scan(
                    out=hs.ap, data0=a_.ap, data1=u_.ap, initial=hprev.ap[:, j:j + 1], op0=ALU.mult, op1=ALU.add),
                    [a_.b, u_.b, hprev.b], [hs.b])
                P.cp("dve", hprev.ap[:, j:j + 1], hs.ap[:, 511:512], [hs.b], [hprev.b])
                P.tt("dve", gT.ap[:, j, :], gy.ap, hs.ap, ALU.mult, [gy.b, hs.b], [gT.b])
                pj += 1
            for tt in range(4):
                for nh in range(2):
                    po = psO[nh]
                    for j in range(8):
                        P.mm(po.ap, gT.ap[:, j, tt * 128:(tt + 1) * 128], w_out.ap[:, j, nh * 512:(nh + 1) * 512],
                             j == 0, j == 7, [gT.b, w_out.b], [po.b])
                    P.tt("dve", ht.ap[:, tt, nh * 512:(nh + 1) * 512], po.ap, ht.ap[:, tt, nh * 512:(nh + 1) * 512],
                         ALU.add, [po.b, ht.b], [ht.b])
            P.dma("sp", hout.ap[r0:r0 + 512, :].rearrange("(tt p) d -> p tt d", p=128), ht.ap,
                  R=[ht.b], W=hout.bufs[ti0:ti0 + 4])
            it += 1


NEG = -1.0e30


def stage_peer(P, I, L, hin, hout, XTd, Gd):
    for s_ in range(BPC):
        peer_route(P, I, L, hin, XTd, Gd, s_)
        peer_main(P, I, L, hin, hout, XTd, Gd, s_)


def peer_route(P, I, L, hin, XTd, Gd, s_):
    P.stage_begin()
    c = load_consts(P, I["consts"])
    identf = P.tile([128, 128], F32)
    P.dma("sp", identf.ap, I["consts"][:, 0:128], W=[identf.b])
    g_bc = P.tile([128, 1024], F32)
    P.dma("sp", g_bc.ap, bcast_rows(I["norm_ffn"][L], D), W=[g_bc.b])
    w_q = P.tile([128, 8, 2048], BF16)
    P.dma("pool", w_q.ap, I["peer_w_q"][L].rearrange("(kc p) n -> p kc n", p=128), W=[w_q.b])
    KT = P.tile([128, 16, 128], BF16)
    ps0, ps1 = P.psb[0], P.psb[1]
    psAB = [P.psb[2], P.psb[3], P.psb[4], P.psb[5]]
    psG = [P.psb[6], P.psb[7]]
    Gq = P.tiles(2, [128, 32, 128], BF16)
    Aq = P.tiles(1, [128, 128, 32], BF16)
    Apq = P.tiles(2, [128, 32, 128], BF16)
    Bg = P.tiles(2, [128, 128, 32], BF16)
    Bp = P.tile([128, 128, 128], BF16)
    Tt = P.tile([128, 8, 16, 32], F32)
    Ksb = Tl(Bp.ap.rearrange("p a b -> p (a b)")[:, 0:2048].rearrange("p (a b) -> p a b", a=16), Bp.b)
    P.dma("pool", Ksb.ap, I["peer_sub_keys"][L].rearrange("h p n d -> n (h p) d"), W=[Ksb.b])
    for half in range(2):
        pv = ps0.ap.bitcast(BF16)
        for q in range(8):
            hp = half * 8 + q
            P.tr(pv[:, q * 128:(q + 1) * 128], Ksb.ap[:, hp, :], c["ident"].ap, [Ksb.b, c["ident"].b], [ps0.b])
        P.cp("act", KT.ap[:, half * 8:(half + 1) * 8, :], pv.rearrange("p (a b) -> p a b", a=8), [ps0.b], [KT.b])
    ht = P.tile([128, 1024], F32)
    hn = P.tile([128, 1024], BF16)
    junk = Tl(Aq[0].ap.rearrange("p a b -> p (a b)")[:, 0:1024], Aq[0].b)
    ss = P.tile([128, 2], F32)
    rstd = P.tile([128, 2], F32)
    XTt = P.tile([128, 8, 128], BF16)
    qT = P.tile([128, 16, 128], BF16)
    scr = P.tiles(2, [128, 256], F32)
    cand = P.tile([128, 8, 16, 16], F32)
    jk16 = P.tile([128, 16], F32)

    class St:
        pass
    sts = []
    for _ in range(2):
        st = St()
        st.s_sb = P.tile([128, 16, 128], F32)
        st.sv = P.tile([128, 16, 16], F32)
        st.cv = P.tile([128, 8, 16], F32)
        st.negm = P.tile([128, 8], F32)
        st.negm1 = P.tile([128, 8], F32)
        st.negm2 = P.tile([128, 8], F32)
        st.Z = P.tile([128, 8], F32)
        st.rZ = P.tile([128, 8], F32)
        st.cc = P.tile([128, 8, 16], F32)
        st.e2 = P.tile([128, 8, 128], F32)
        st.cT = P.tile([128, 128], F32)
        st.sv4 = st.sv.ap.rearrange("p (h two) a -> p h two a", two=2)
        st.s4 = st.s_sb.ap.rearrange("p (h two) n -> p h two n", two=2)
        sts.append(st)
    cnt = {"ev": 0, "nb": 0}

    def front(tl_):
        st = sts[tl_ % 2]
        ti = s_ * 16 + tl_
        r0 = ti * 128
        s_sb, sv, cv = st.s_sb, st.sv, st.cv

        def fa():
            P.dma("sp", ht.ap, hin.ap[r0:r0 + 128, :], R=[hin.bufs[ti]], W=[ht.b])
            rms_rstd(P, c, ht.ap, ht.b, junk, ss, rstd, 0)
            P.stt("dve", hn.ap, ht.ap, rstd.ap[:, 0:1], g_bc.ap, ALU.mult, ALU.mult, [ht.b, rstd.b, g_bc.b], [hn.b])
            transpose_tile(P, c, hn, XTt.ap, XTt.b, 8, ps0, "act")
            P.dma("sp", XTd.ap[:, :, r0:r0 + 128], XTt.ap, R=[XTt.b], W=[XTd.bufs[ti]])
            for g in range(4):
                pq = ps1
                for u in range(4):
                    hp = g * 4 + u
                    for kc in range(8):
                        P.mm(pq.ap[:, u * 128:(u + 1) * 128], w_q.ap[:, kc, hp * 128:(hp + 1) * 128], XTt.ap[:, kc, :],
                             kc == 0, kc == 7, [w_q.b, XTt.b], [pq.b])
                P.cp("act", qT.ap[:, g * 4:(g + 1) * 4, :], pq.ap.rearrange("p (a b) -> p a b", a=4), [pq.b], [qT.b])
            for g in range(4):
                pq = ps0
                for u in range(4):
                    hp = g * 4 + u
                    P.mm(pq.ap[:, u * 128:(u + 1) * 128], qT.ap[:, hp, :], KT.ap[:, hp, :], True, True, [qT.b, KT.b], [pq.b])
                P.cp("act", s_sb.ap[:, g * 4:(g + 1) * 4, :], pq.ap.rearrange("p (a b) -> p a b", a=4), [pq.b], [s_sb.b])

        def topk(hps):
            for hp in hps:
                sc = scr[hp % 2]
                P.op("dve", lambda e, hp=hp: e.max(out=sv.ap[:, hp, 0:8], in_=s_sb.ap[:, hp, :]), [s_sb.b], [sv.b])
                P.op("dve", lambda e, hp=hp, sc=sc: e.match_replace(out=sc.ap[:, 0:128], in_to_replace=sv.ap[:, hp, 0:8],
                                                                   in_values=s_sb.ap[:, hp, :], imm_value=NEG),
                     [s_sb.b, sv.b], [sc.b])
                P.op("dve", lambda e, hp=hp, sc=sc: e.max(out=sv.ap[:, hp, 8:16], in_=sc.ap[:, 0:128]), [sc.b], [sv.b])

        def fd():
            P.tt("dve", cand.ap, st.sv4[:, :, 0, :].unsqueeze(3).to_broadcast([128, 8, 16, 16]),
                 st.sv4[:, :, 1, :].unsqueeze(2).to_broadcast([128, 8, 16, 16]), ALU.add, [sv.b], [cand.b])
            for h in range(8):
                sc = scr[h % 2]
                cflat = cand.ap[:, h, :, :].rearrange("p a b -> p (a b)")
                P.op("dve", lambda e, h=h, cflat=cflat: e.max(out=cv.ap[:, h, 0:8], in_=cflat), [cand.b], [cv.b])
                P.op("dve", lambda e, h=h, sc=sc, cflat=cflat: e.match_replace(out=sc.ap, in_to_replace=cv.ap[:, h, 0:8],
                                                                              in_values=cflat, imm_value=NEG),
                     [cand.b, cv.b], [sc.b])
                P.op("dve", lambda e, h=h, sc=sc: e.max(out=cv.ap[:, h, 8:16], in_=sc.ap), [sc.b], [cv.b])

        def fe():
            negm, negm1, negm2, Z, rZ, cc, e2, cT = st.negm, st.negm1, st.negm2, st.Z, st.rZ, st.cc, st.e2, st.cT
            P.ts("dve", negm1.ap, st.sv4[:, :, 0, 0], -1.0, None, ALU.mult, None, [sv.b], [negm1.b])
            P.ts("dve", negm2.ap, st.sv4[:, :, 1, 0], -1.0, None, ALU.mult, None, [sv.b], [negm2.b])
            P.tt("dve", negm.ap, negm1.ap, negm2.ap, ALU.add, [negm1.b, negm2.b], [negm.b])
            for h in range(8):
                P.act(jk16.ap, cv.ap[:, h, :], AF.Exp, R=[cv.b, negm.b], W=[jk16.b, Z.b], bias=negm.ap[:, h:h + 1],
                      accum_out=Z.ap[:, h:h + 1])
            P.op("dve", lambda e: e.reciprocal(out=rZ.ap, in_=Z.ap), [Z.b], [rZ.b])
            for h in range(8):
                P.act(cc.ap[:, h, :], sv.ap[:, 2 * h, :], AF.Exp, R=[sv.b, negm1.b], W=[cc.b], bias=negm1.ap[:, h:h + 1])
            for h in range(8):
                P.act(e2.ap[:, h, :], s_sb.ap[:, 2 * h + 1, :], AF.Exp, R=[s_sb.b, negm2.b], W=[e2.b], bias=negm2.ap[:, h:h + 1])
            P.tt("dve", cc.ap, cc.ap, rZ.ap.unsqueeze(2).to_broadcast([128, 8, 16]), ALU.mult, [cc.b, rZ.b], [cc.b])
            P.tr(ps1.ap[:, 0:128], cc.ap.rearrange("p h a -> p (h a)"), identf.ap, [cc.b, identf.b], [ps1.b])
            P.cp("act", cT.ap, ps1.ap[:, 0:128], [ps1.b], [cT.b])

        return [fa, lambda: topk(range(0, 8)), lambda: topk(range(8, 16)), fd, fe]

    def back(tl_):
        st = sts[tl_ % 2]
        ti = s_ * 16 + tl_
        s_sb, sv, cv, e2, cT, s4, sv4 = st.s_sb, st.sv, st.cv, st.e2, st.cT, st.s4, st.sv4

        def bgrp(g):
            bg = Bg[g % 2]
            i0 = g * 32
            P.tt("pool", Tt.ap, s4[:, :, 1, i0:i0 + 32].unsqueeze(2).to_broadcast([128, 8, 16, 32]),
                 sv4[:, :, 0, :].unsqueeze(3).to_broadcast([128, 8, 16, 32]), ALU.add, [s_sb.b, sv.b], [Tt.b])
            P.tt("dve", Tt.ap, Tt.ap, cv.ap[:, :, 15:16].unsqueeze(3).to_broadcast([128, 8, 16, 32]), ALU.is_ge,
                 [Tt.b, cv.b], [Tt.b])
            P.tt("pool", bg.ap.rearrange("p (h a) i -> p h a i", h=8), Tt.ap,
                 e2.ap[:, :, i0:i0 + 32].unsqueeze(2).to_broadcast([128, 8, 16, 32]), ALU.mult, [Tt.b, e2.b], [bg.b])
            for blk in range(4):
                pb = psAB[cnt["nb"] % 4]
                cnt["nb"] += 1
                pbv = pb.ap.bitcast(BF16)
                for ii in range(8):
                    P.tr(pbv[:, ii * 128:(ii + 1) * 128], bg.ap[:, :, blk * 8 + ii], c["ident"].ap,
                         [bg.b, c["ident"].b], [pb.b])
                P.cp("act", Bp.ap[:, i0 + blk * 8:i0 + blk * 8 + 8, :],
                     pbv.rearrange("p (a b) -> p a b", a=8), [pb.b], [Bp.b])

        def aqr(q4):
            aq = Aq[0]
            apq = Apq[q4 % 2]
            gq = Gq[q4 % 2]
            i0 = q4 * 32
            P.tt("dve", aq.ap.rearrange("p (h a) i -> p h a i", h=8),
                 s4[:, :, 0, i0:i0 + 32].unsqueeze(2).to_broadcast([128, 8, 16, 32]),
                 sv4[:, :, 0, :].unsqueeze(3).to_broadcast([128, 8, 16, 32]), ALU.is_equal, [s_sb.b, sv.b], [aq.b])
            for blk in range(4):
                pb = psAB[cnt["nb"] % 4]
                cnt["nb"] += 1
                pbv = pb.ap.bitcast(BF16)
                for ii in range(8):
                    P.tr(pbv[:, ii * 128:(ii + 1) * 128], aq.ap[:, :, blk * 8 + ii], c["ident"].ap,
                         [aq.b, c["ident"].b], [pb.b])
                P.tt("dve", apq.ap[:, blk * 8:blk * 8 + 8, :], pbv.rearrange("p (a b) -> p a b", a=8),
                     cT.ap.unsqueeze(1).to_broadcast([128, 8, 128]), ALU.mult, [pb.b, cT.b], [apq.b])
            for tg in range(8):
                pg = psG[tg % 2]
                for tt in range(16):
                    t = tg * 16 + tt
                    P.mm(pg.ap[:, tt * 32:(tt + 1) * 32], Bp.ap[:, :, t], apq.ap[:, :, t], True, True,
                         [Bp.b, apq.b], [pg.b])
                P.cp("act", gq.ap[:, :, tg * 16:(tg + 1) * 16],
                     pg.ap.rearrange("p (t i) -> p i t", t=16), [pg.b], [gq.b])
            for hh in range(2):
                a0 = i0 + hh * 16
                P.dma("sp", Gd.ap[ti, a0:a0 + 16].rearrange("a b t -> b a t"), gq.ap[:, hh * 16:(hh + 1) * 16, :],
                      R=[gq.b], W=[Gd.bufs[ti]])

        return [lambda: bgrp(0), lambda: bgrp(1), lambda: bgrp(2), lambda: bgrp(3),
                lambda: aqr(0), lambda: aqr(1), lambda: aqr(2), lambda: aqr(3)]

    for f in front(0):
        f()
    for tl_ in range(16):
        bk = back(tl_)
        fr = front(tl_ + 1) if tl_ + 1 < 16 else []
        order = [bk[0]] + fr[0:1] + [bk[1]] + fr[1:2] + [bk[2]] + fr[2:3] + [bk[3]] + fr[3:4] + [bk[4]] + fr[4:5] + bk[5:]
        for f in order:
            f()


def peer_main(P, I, L, hin, hout, XTd, Gd, s_):
    P.stage_begin()
    c = load_consts(P, I["consts"])
    XT = P.tile([128, 8, S], BF16)
    t0 = s_ * 16
    P.dma("sp", XT.ap, XTd.ap[:, :, s_ * S:(s_ + 1) * S], R=XTd.bufs[t0:t0 + 16], W=[XT.b])
    oacc = P.tile([128, 16, 1024], F32)
    oaccb = [Buf() for _ in range(16)]
    NB = 4
    Ust = P.tiles(2, [128, 1024], BF16)
    UT = [P.tiles(NB, [128, 8, 128], BF16) for _ in range(2)]
    Vb = [P.tiles(NB, [128, 1024], BF16) for _ in range(2)]
    Gb = [P.tiles(NB, [128, 16, 128], BF16) for _ in range(2)]
    actb = P.tiles(3, [128, 256], BF16)
    Wb = P.tiles(3, [128, 256], BF16)
    psS = [P.psb[0], P.psb[1]]
    psT = P.psb[2]
    psO = [P.psb[3], P.psb[4], P.psb[5], P.psb[6], P.psb[7]]
    ngrp = 128 // NB

    def load_group(grp):
        par = grp % 2
        for bi in range(NB):
            i1 = grp * NB + bi
            ust = Ust[bi % 2]
            P.dma("pool", ust.ap, I["peer_u"][L, i1 * 128:(i1 + 1) * 128, :], W=[ust.b])
            P.dma("pool", Vb[par][bi].ap, I["peer_v"][L, i1 * 128:(i1 + 1) * 128, :], W=[Vb[par][bi].b])
            P.dma("sp", Gb[par][bi].ap, Gd.ap[t0:t0 + 16, i1].rearrange("n b t -> b n t"),
                  R=Gd.bufs[t0:t0 + 16], W=[Gb[par][bi].b])
            pv = psT.ap.bitcast(BF16)
            for dc in range(8):
                P.tr(pv[:, dc * 128:(dc + 1) * 128], ust.ap[:, dc * 128:(dc + 1) * 128], c["ident"].ap,
                     [ust.b, c["ident"].b], [psT.b])
            P.cp("act", UT[par][bi].ap, pv.rearrange("p (a b) -> p a b", a=8), [psT.b], [UT[par][bi].b])

    load_group(0)
    wi = 0
    for grp in range(ngrp):
        par = grp % 2
        if grp + 1 < ngrp:
            load_group(grp + 1)
        items = [(pair, bi) for pair in range(8) for bi in range(NB)]
        pend = None

        def emit_S(pair, bi, k):
            ps = psS[k % 2]
            for dc in range(8):
                P.mm(ps.ap[:, 0:256], UT[par][bi].ap[:, dc, :], XT.ap[:, dc, pair * 256:(pair + 1) * 256], dc == 0, dc == 7,
                     [UT[par][bi].b, XT.b], [ps.b])
            a_ = actb[k % 3]
            w_ = Wb[k % 3]
            P.act(a_.ap, ps.ap[:, 0:256], AF.Gelu_apprx_tanh, R=[ps.b], W=[a_.b])
            P.tt("pool" if k % 2 == 0 else "dve", w_.ap, a_.ap,
                 Gb[par][bi].ap[:, 2 * pair:2 * pair + 2, :].rearrange("p a b -> p (a b)"), ALU.mult,
                 [a_.b, Gb[par][bi].b], [w_.b])
            return w_

        def emit_O(pair, bi, w_):
            for tt in range(2):
                for nh in range(2):
                    po = psO[(4 * (grp * 8 + pair) + tt * 2 + nh) % 5]
                    P.mm(po.ap, w_.ap[:, tt * 128:(tt + 1) * 128], Vb[par][bi].ap[:, nh * 512:(nh + 1) * 512],
                         bi == 0, bi == NB - 1, [w_.b, Vb[par][bi].b], [po.b])
            if bi == NB - 1:
                for tt in range(2):
                    tix = 2 * pair + tt
                    for nh in range(2):
                        po = psO[(4 * (grp * 8 + pair) + tt * 2 + nh) % 5]
                        dst = oacc.ap[:, tix, nh * 512:(nh + 1) * 512]
                        if grp == 0:
                            P.cp("dve", dst, po.ap, [po.b], [oaccb[tix]])
                        else:
                            P.tt("dve", dst, po.ap, dst, ALU.add, [po.b, oaccb[tix]], [oaccb[tix]])

        for (pair, bi) in items:
            w_ = emit_S(pair, bi, wi)
            wi += 1
            if pend is not None:
                emit_O(*pend)
            pend = (pair, bi, w_)
        emit_O(*pend)
    hts = P.tiles(2, [128, 1024], F32)
    for tl_ in range(16):
        ti = t0 + tl_
        ht = hts[tl_ % 2]
        P.dma("sp", ht.ap, hin.ap[ti * 128:(ti + 1) * 128, :], R=[hin.bufs[ti]], W=[ht.b])
        P.tt("dve", ht.ap, ht.ap, oacc.ap[:, tl_, :], ALU.add, [ht.b, oaccb[tl_]], [ht.b])
        P.dma("sp", hout.ap[ti * 128:(ti + 1) * 128, :], ht.ap, R=[ht.b], W=[hout.bufs[ti]])


def stage_ple(P, I, L, hin, hout, final):
    P.stage_begin()
    c = load_consts(P, I["consts"])
    w_g = P.tile([128, 8, 1024], BF16)
    P.dma("pool", w_g.ap, I["ple_w_gate"][L].rearrange("(kc p) n -> p kc n", p=128), W=[w_g.b])
    w_p = P.tile([128, 2, 1024], BF16)
    P.dma("pool", w_p.ap, I["ple_w_proj"][L].rearrange("(kc p) n -> p kc n", p=128), W=[w_p.b])
    g_bc = P.tile([128, 1024], F32)
    P.dma("sp", g_bc.ap, bcast_rows(I["norm_ple"][L], D), W=[g_bc.b])
    gf_bc = P.tile([128, 1024], F32)
    P.dma("sp", gf_bc.ap, bcast_rows(I["final_norm"], D), W=[gf_bc.b])
    pflat = I["p"][L].rearrange("b s d -> (b s) d")
    hts = P.tiles(3, [128, 1024], F32)
    pts = P.tiles(2, [128, 256], F32)
    hns = P.tiles(2, [128, 1024], BF16)
    pbs = P.tiles(2, [128, 256], BF16)
    hnTs = P.tiles(2, [128, 8, 128], BF16)
    pTs = P.tiles(2, [128, 2, 128], BF16)
    sgs = P.tiles(2, [128, 512], F32)
    junk = P.tile([128, 1024], BF16)
    ss = P.tile([128, 2], F32)
    rstd = P.tile([128, 2], F32)
    outs = P.tiles(2, [128, 1024], F32)
    for ti in range(NT):
        ht = hts[ti % 3]
        pt = pts[ti % 2]
        hn = hns[ti % 2]
        pb = pbs[ti % 2]
        hnT = hnTs[ti % 2]
        pT = pTs[ti % 2]
        r0 = ti * 128
        P.dma("sp", ht.ap, hin.ap[r0:r0 + 128, :], R=[hin.bufs[ti]], W=[ht.b])
        P.dma("sp", pt.ap, pflat[r0:r0 + 128, :], W=[pt.b])
        rms_rstd(P, c, ht.ap, ht.b, junk, ss, rstd, 0)
        P.stt("dve", hn.ap, ht.ap, rstd.ap[:, 0:1], g_bc.ap, ALU.mult, ALU.mult, [ht.b, rstd.b, g_bc.b], [hn.b])
        transpose_tile(P, c, hn, hnT.ap, hnT.b, 8, P.psb[0], "act")
        P.cp("pool", pb.ap, pt.ap, [pt.b], [pb.b])
        transpose_tile(P, c, pb, pT.ap, pT.b, 2, P.psb[1], "act")
        for nh in range(2):
            pg = P.psb[2 + nh]
            pp = P.psb[4 + nh]
            sg = sgs[nh]
            for kc in range(8):
                P.mm(pg.ap, hnT.ap[:, kc, :], w_g.ap[:, kc, nh * 512:(nh + 1) * 512], kc == 0, kc == 7, [hnT.b, w_g.b], [pg.b])
            for kc in range(2):
                P.mm(pp.ap, pT.ap[:, kc, :], w_p.ap[:, kc, nh * 512:(nh + 1) * 512], kc == 0, kc == 1, [pT.b, w_p.b], [pp.b])
            P.act(sg.ap, pg.ap, AF.Sigmoid, R=[pg.b], W=[sg.b])
            P.tt("dve", sg.ap, sg.ap, pp.ap, ALU.mult, [sg.b, pp.b], [sg.b])
            P.tt("dve", ht.ap[:, nh * 512:(nh + 1) * 512], ht.ap[:, nh * 512:(nh + 1) * 512], sg.ap, ALU.add, [ht.b, sg.b], [ht.b])
        if final:
            ot = outs[ti % 2]
            rms_rstd(P, c, ht.ap, ht.b, junk, ss, rstd, 1)
            P.stt("dve", ot.ap, ht.ap, rstd.ap[:, 1:2], gf_bc.ap, ALU.mult, ALU.mult, [ht.b, rstd.b, gf_bc.b], [ot.b])
            P.dma("sp", hout.ap[r0:r0 + 128, :], ot.ap, R=[ot.b], W=[hout.bufs[ti]])
        else:
            P.dma("sp", hout.ap[r0:r0 + 128, :], ht.ap, R=[ht.b], W=[hout.bufs[ti]])


def stage_attn(P, I, hin, hout):
    for s_ in range(BPC):
        attn_seq(P, I, hin, hout, s_)


def attn_seq(P, I, hin, hout, s_):
    P.stage_begin()
    c = load_consts(P, I["consts"])
    negtri = P.tile([128, 128], BF16)
    P.dma("pool", negtri.ap, I["consts"][:, 128:256], W=[negtri.b])
    negones = P.tile([128, 128], BF16)
    P.dma("pool", negones.ap, I["consts"][:, 256:384], W=[negones.b])
    wm = P.tile([128, 896], BF16)
    P.dma("pool", wm.ap, I["consts"][:, 384:384 + 896], W=[wm.b])
    qT = P.tile([128, 8, S], BF16)
    kT = P.tile([128, 8, S], BF16)
    v = P.tile([128, 16, 1024], BF16)
    qb = [[[Buf(), Buf()] for _ in range(4)] for _ in range(8)]
    mark = P.aoff
    w_q = P.tile([128, 8, 1024], BF16)
    P.dma("pool", w_q.ap, I["b_w_q"][0].rearrange("(kc p) n -> p kc n", p=128), W=[w_q.b])
    w_kv = P.tile([128, 8, 2048], BF16)
    P.dma("pool", w_kv.ap, I["w_kv"].rearrange("(kc p) n -> p kc n", p=128), W=[w_kv.b])
    g1 = P.tile([128, 1024], F32)
    P.dma("sp", g1.ap, bcast_rows(I["norm_mix"][1], D), W=[g1.b])
    g2 = P.tile([128, 1024], F32)
    P.dma("sp", g2.ap, bcast_rows(I["kv_norm"], D), W=[g2.b])
    ht = P.tile([128, 4, 1024], F32)
    hn = P.tile([128, 1024], BF16)
    hkv = P.tile([128, 1024], BF16)
    hnT = P.tile([128, 8, 512], BF16)
    hkvT = P.tile([128, 8, 512], BF16)
    junk = P.tile([128, 1024], BF16)
    ss = P.tile([128, 4], F32)
    rstd = P.tile([128, 4], F32)
    for ch in range(4):
        r0 = s_ * S + ch * 512
        ti0 = r0 // 128
        P.dma("sp", ht.ap, hin.ap[r0:r0 + 512, :].rearrange("(tt p) d -> p tt d", p=128), R=hin.bufs[ti0:ti0 + 4], W=[ht.b])
        for tt in range(4):
            rms_rstd(P, c, ht.ap[:, tt, :], ht.b, junk, ss, rstd, tt)
            P.stt("dve", hn.ap, ht.ap[:, tt, :], rstd.ap[:, tt:tt + 1], g1.ap, ALU.mult, ALU.mult, [ht.b, rstd.b, g1.b], [hn.b])
            P.stt("dve", hkv.ap, ht.ap[:, tt, :], rstd.ap[:, tt:tt + 1], g2.ap, ALU.mult, ALU.mult, [ht.b, rstd.b, g2.b], [hkv.b])
            transpose_tile(P, c, hn, hnT.ap[:, :, tt * 128:(tt + 1) * 128], hnT.b, 8, P.psb[4], "act")
            transpose_tile(P, c, hkv, hkvT.ap[:, :, tt * 128:(tt + 1) * 128], hkvT.b, 8, P.psb[5], "act")
        for jc in range(8):
            pq = P.psb[6]
            for kc in range(8):
                P.mm(pq.ap, w_q.ap[:, kc, jc * 128:(jc + 1) * 128], hnT.ap[:, kc, :], kc == 0, kc == 7, [w_q.b, hnT.b], [pq.b])
            P.act(qT.ap[:, jc, ch * 512:(ch + 1) * 512], pq.ap, AF.Copy, R=[pq.b], W=qb[jc][ch], scale=0.125)
            pk = P.psb[7]
            for kc in range(8):
                P.mm(pk.ap, w_kv.ap[:, kc, jc * 128:(jc + 1) * 128], hkvT.ap[:, kc, :], kc == 0, kc == 7, [w_kv.b, hkvT.b], [pk.b])
            P.cp("dve", kT.ap[:, jc, ch * 512:(ch + 1) * 512], pk.ap, [pk.b], [kT.b])
        for tt in range(4):
            for nh in range(2):
                pv_ = P.psb[6 + nh]
                for kc in range(8):
                    P.mm(pv_.ap, hkvT.ap[:, kc, tt * 128:(tt + 1) * 128], w_kv.ap[:, kc, 1024 + nh * 512:1024 + (nh + 1) * 512],
                         kc == 0, kc == 7, [hkvT.b, w_kv.b], [pv_.b])
                P.cp("act" if nh == 0 else "dve", v.ap[:, ch * 4 + tt, nh * 512:(nh + 1) * 512], pv_.ap, [pv_.b], [v.b])
    P.barrier()
    P.aoff = mark
    w_o = P.tile([128, 8, 1024], BF16)
    P.dma("pool", w_o.ap, I["b_w_o"][0].rearrange("(kc p) n -> p kc n", p=128), W=[w_o.b])
    es = P.tiles(3, [128, 512], F32)
    sps = P.tiles(4, [128, 512], BF16)
    As = P.tiles(3, [128, 512], BF16)
    Srun = P.tile([128, 512], BF16)
    psZs = [P.psb[0], P.psb[1], P.psb[2], P.psb[3]]
    psOs = [P.psb[4], P.psb[5]]
    pairs = []
    no = 0
    for h in range(16):
        for cq in range(4):
            top = 4 * cq + 3
            pso = psOs[no % 2]
            no += 1
            for kb in range(top, -1, -1):
                pairs.append((h, cq, kb, top, pso))

    def qk(n):
        h, cq, kb, top, pso = pairs[n]
        jc, par = h // 2, h % 2
        po = par * 64
        psZ = psZs[n % 4]
        P.mm(psZ.ap, kT.ap[po:po + 64, jc, kb * 128:(kb + 1) * 128], qT.ap[po:po + 64, jc, cq * 512:(cq + 1) * 512],
             True, False, [kT.b, qb[jc][cq][par]], [psZ.b])

    def xa(n):
        h, cq, kb, top, pso = pairs[n]
        psZ, e_, sp = psZs[n % 4], es[n % 3], sps[n % 4]
        P.act(e_.ap, psZ.ap, AF.Exp, R=[psZ.b], W=[e_.b])
        P.act(sp.ap, e_.ap, AF.Ln, R=[e_.b, c["one"].b], W=[sp.b], bias=c["one"].ap)
        if kb >= 4 * cq:
            dl = cq * 512 - kb * 128
            P.tt("dve", sp.ap, sp.ap, wm.ap[:, 384 + dl:384 + dl + 512], ALU.mult, [sp.b, wm.b], [sp.b])

    def to(n):
        h, cq, kb, top, pso = pairs[n]
        psZ, sp = psZs[n % 4], sps[n % 4]
        P.mm(psZ.ap, negtri.ap, sp.ap, False, kb == top, [negtri.b, sp.b], [psZ.b])
        if kb != top:
            P.mm(psZ.ap, negones.ap, Srun.ap, False, True, [negones.b, Srun.b], [psZ.b])
        if kb == top:
            P.cp("pool", Srun.ap, sp.ap, [sp.b], [Srun.b])
        elif kb > 0:
            P.tt("pool", Srun.ap, Srun.ap, sp.ap, ALU.add, [Srun.b, sp.b], [Srun.b])

    def ea(n):
        h, cq, kb, top, pso = pairs[n]
        psZ, A_ = psZs[n % 4], As[n % 3]
        P.act(A_.ap, psZ.ap, AF.Exp, R=[psZ.b], W=[A_.b])
        if kb >= 4 * cq:
            dl = cq * 512 - kb * 128
            P.tt("dve", A_.ap, A_.ap, wm.ap[:, 384 + dl:384 + dl + 512], ALU.mult, [A_.b, wm.b], [A_.b])

    def av(n):
        h, cq, kb, top, pso = pairs[n]
        jc, par = h // 2, h % 2
        po = par * 64
        A_ = As[n % 3]
        P.mm(pso.ap[po:po + 64, :], v.ap[:, kb, h * 64:(h + 1) * 64], A_.ap, kb == top, kb == 0, [v.b, A_.b], [pso.b])
        if kb == 0:
            P.cp("dve", qT.ap[po:po + 64, jc, cq * 512:(cq + 1) * 512], pso.ap[po:po + 64, :], [pso.b], [qb[jc][cq][par]])

    NP_ = len(pairs)
    qk(0)
    qk(1)
    qk(2)
    xa(0)
    xa(1)
    for n in range(NP_):
        to(n)
        if n + 2 < NP_:
            xa(n + 2)
        ea(n)
        if n + 3 < NP_:
            qk(n + 3)
        if n >= 1:
            av(n - 1)
    av(NP_ - 1)
    hts = P.tiles(2, [128, 1024], F32)
    for tl_ in range(16):
        ti = s_ * 16 + tl_
        cq = tl_ // 4
        ht2 = hts[tl_ % 2]
        P.dma("sp", ht2.ap, hin.ap[ti * 128:(ti + 1) * 128, :], R=[hin.bufs[ti]], W=[ht2.b])
        for nh in range(2):
            po_ = P.psb[6 + nh]
            for jc in range(8):
                P.mm(po_.ap, qT.ap[:, jc, tl_ * 128:(tl_ + 1) * 128], w_o.ap[:, jc, nh * 512:(nh + 1) * 512], jc == 0, jc == 7,
                     [qb[jc][cq][0], qb[jc][cq][1], w_o.b], [po_.b])
            P.tt("dve", ht2.ap[:, nh * 512:(nh + 1) * 512], ht2.ap[:, nh * 512:(nh + 1) * 512], po_.ap, ALU.add, [ht2.b, po_.b], [ht2.b])
        P.dma("sp", hout.ap[ti * 128:(ti + 1) * 128, :], ht2.ap, R=[ht2.b], W=[hout.bufs[ti]])


WEIGHT_SPECS = [
    ("norm_mix", [2, 1024]), ("a_w_in", [1, 1024, 2048]), ("a_conv_w", [1, 4, 1024]), ("a_conv_b", [1, 1024]),
    ("a_w_r", [1, 8, 128, 128]), ("a_w_i", [1, 8, 128, 128]), ("a_b_r", [1, 1024]), ("a_b_i", [1, 1024]),
    ("a_lambda", [1, 1024]), ("a_w_out", [1, 1024, 1024]), ("kv_norm", [1024]), ("w_kv", [1024, 2048]),
    ("b_w_q", [1, 1024, 1024]), ("b_w_o", [1, 1024, 1024]), ("norm_ffn", [2, 1024]), ("peer_w_q", [2, 1024, 2048]),
    ("peer_sub_keys", [2, 8, 2, 128, 128]), ("peer_u", [2, 16384, 1024]), ("peer_v", [2, 16384, 1024]),
    ("norm_ple", [2, 1024]), ("ple_w_gate", [2, 1024, 1024]), ("ple_w_proj", [2, 256, 1024]), ("final_norm", [1024]),
]
NCONST = 128 * 3 + 896


def make_consts():
    cst = np.zeros((128, NCONST), np.float32)
    cst[:, 0:128] = np.eye(128, dtype=np.float32)
    j = np.arange(128)[:, None]
    s_ = np.arange(128)[None, :]
    cst[:, 128:256] = -(j >= s_).astype(np.float32)
    cst[:, 256:384] = -1.0
    xx = np.arange(896)[None, :]
    cst[:, 384:384 + 896] = (j < xx - 384).astype(np.float32)
    return cst


def build(stages):
    nc = bass.Bass("TRN2", target_bir_lowering=False)
    I = {}
    I["x"] = nc.dram_tensor("x", [BPC, S, D], F32, kind="ExternalInput").ap()
    I["p"] = nc.dram_tensor("p", [2, BPC, S, PLE], F32, kind="ExternalInput").ap()
    for nm, shp in WEIGHT_SPECS:
        I[nm] = nc.dram_tensor(nm, shp, F32, kind="ExternalInput").ap()
    I["consts"] = nc.dram_tensor("consts", [128, NCONST], F32, kind="ExternalInput").ap()
    out = nc.dram_tensor("out", [T, D], F32, kind="ExternalOutput").ap()
    P = Prog(nc)
    hx = Dram(I["x"].rearrange("b s d -> (b s) d"), NT)
    hs = [Dram(nc.dram_tensor("hscr%d" % i, [T, D], F32, kind="Internal").ap(), NT) for i in range(2)]
    XTd = Dram(nc.dram_tensor("xtd", [128, 8, T], BF16, kind="Internal").ap(), NT)
    Gd = Dram(nc.dram_tensor("gd", [NT, 128, 128, 128], BF16, kind="Internal").ap(), NT)
    hout = Dram(out, NT)
    cur = hx
    nst = len(stages)
    for si, st in enumerate(stages):
        dst = hout if si == nst - 1 else hs[si % 2]
        if st == "rglru":
            stage_rglru(P, I, cur, dst)
        elif st in ("peer0", "peer1"):
            stage_peer(P, I, int(st[-1]), cur, dst, XTd, Gd)
        elif st in ("ple0", "ple1"):
            stage_ple(P, I, int(st[-1]), cur, dst, st == "ple1")
        elif st == "attn":
            stage_attn(P, I, cur, dst)
        else:
            raise ValueError(st)
        cur = dst
    P.barrier()
    return nc, P


ALL_STAGES = ["rglru", "peer0", "ple0", "attn", "peer1", "ple1"]


def kernel(**inputs):
    nc, P = build(ALL_STAGES)
    cst = make_consts()
    in_maps = []
    for c in range(NCORES):
        m = {"consts": cst}
        m["x"] = np.ascontiguousarray(inputs["x"][c * BPC:(c + 1) * BPC])
        m["p"] = np.ascontiguousarray(inputs["p"][:, c * BPC:(c + 1) * BPC])
        for nm, _ in WEIGHT_SPECS:
            m[nm] = np.ascontiguousarray(inputs[nm])
        in_maps.append(m)
    res = run_bass_kernel_spmd(nc, in_maps, core_ids=list(range(NCORES)))
    outs = [np.asarray(r["out"]).reshape(BPC, S, D) for r in res.results]
    return np.concatenate(outs, axis=0).astype(np.float32)
```
